# Optimizing a Trainium2 kernel written in Bass

```python
import jax, jax.numpy as jnp
from jax import lax
import numpy as np

D_MODEL = 1024
BATCH = 8
SEQ = 2048
DEPTH = 1

GRID_W = 64
CTX_LEN = 256
HEAD_DIM = 64
ATTN_Q_HEADS = 8
ATTN_KV_HEADS = 2
ATTN_GROUP = ATTN_Q_HEADS // ATTN_KV_HEADS
WINDOW = 128
ATTN_BLOCK = 128
ROPE_BASE = 10000.0
DN_HEADS = 8
DN_HEAD_DIM = 64
CONV_W = 3
CHUNK = 64
N_DIR = 2
D_FF = 4 * D_MODEL
EPS = 1e-6
NEG_INF = -1e30

ATTN_WIDTH = ATTN_Q_HEADS * HEAD_DIM
KV_WIDTH = ATTN_KV_HEADS * HEAD_DIM
DN_WIDTH = DN_HEADS * DN_HEAD_DIM
IN_SPLITS = (ATTN_WIDTH, KV_WIDTH, KV_WIDTH, DN_WIDTH, DN_WIDTH, DN_WIDTH, DN_WIDTH,
             N_DIR * DN_HEADS, N_DIR * DN_HEADS, D_MODEL, D_MODEL)
IN_WIDTH = 4 * DN_WIDTH + ATTN_WIDTH + 2 * KV_WIDTH + 2 * N_DIR * DN_HEADS + 2 * D_MODEL

kernel_name = "hybrid_swa_gdn_dit_layer"


def rms_norm(x, g):
    xf = x.astype(jnp.float32)
    return xf * lax.rsqrt(jnp.mean(xf * xf, axis=-1, keepdims=True) + EPS) * g.astype(jnp.float32)


def l2_normalize(x):
    return x * lax.rsqrt(jnp.sum(x * x, axis=-1, keepdims=True) + EPS)


def split_cols(p):
    offsets = [int(o) for o in np.cumsum(IN_SPLITS)[:-1]]
    return jnp.split(p, offsets, axis=-1)


def axial_rope(x, row, col):
    half = HEAD_DIM // 2
    n_freq = half // 2
    freqs = ROPE_BASE ** (-jnp.arange(n_freq, dtype=jnp.float32) / n_freq)

    def rot(xh, pos):
        ang = pos.astype(jnp.float32)[:, None] * freqs
        cos = jnp.cos(ang)[None, :, None, :]
        sin = jnp.sin(ang)[None, :, None, :]
        x1, x2 = xh[..., :n_freq], xh[..., n_freq:]
        return jnp.concatenate([x1 * cos - x2 * sin, x1 * sin + x2 * cos], axis=-1)

    return jnp.concatenate([rot(x[..., :half], row), rot(x[..., half:], col)], axis=-1)


def centred_short_conv(x, w):
    pad = CONV_W // 2
    T = x.shape[1]
    xp = jnp.pad(x, ((0, 0), (pad, CONV_W - 1 - pad), (0, 0)))
    y = sum(xp[:, tap:tap + T] * w[tap] for tap in range(CONV_W))
    return jax.nn.silu(y)


def gated_delta_chunked(q, k, v, g_log, beta, s0):
    B, T, H, dk = q.shape
    n = T // CHUNK

    def chunks(a):
        return jnp.moveaxis(a.reshape((B, n, CHUNK, H) + a.shape[3:]), 3, 1)

    qc, kc, vc = chunks(q), chunks(k), chunks(v)
    gc, bc = chunks(g_log), chunks(beta)
    G = jnp.cumsum(gc, axis=-1)
    idx = jnp.arange(CHUNK)
    strict = idx[:, None] > idx[None, :]
    incl = idx[:, None] >= idx[None, :]
    diff = G[..., :, None] - G[..., None, :]
    dec_strict = jnp.where(strict, jnp.exp(jnp.where(strict, diff, 0.0)), 0.0)
    dec_incl = jnp.where(incl, jnp.exp(jnp.where(incl, diff, 0.0)), 0.0)
    kk = jnp.einsum('bhnid,bhnjd->bhnij', kc, kc)
    t_mat = jnp.eye(CHUNK, dtype=jnp.float32) + bc[..., :, None] * dec_strict * kk
    eG = jnp.exp(G)
    w_mat = lax.linalg.triangular_solve(t_mat, (bc * eG)[..., None] * kc, left_side=True, lower=True,
                                        unit_diagonal=True)
    u_v = lax.linalg.triangular_solve(t_mat, bc[..., None] * vc, left_side=True, lower=True,
                                      unit_diagonal=True)
    intra = dec_incl * jnp.einsum('bhnid,bhnjd->bhnij', qc, kc)
    q_dec = eG[..., None] * qc
    g_last = G[..., -1]
    k_tail = jnp.exp(g_last[..., None] - G)[..., None] * kc

    def step(S, xs):
        w_c, uv_c, qd_c, intra_c, kt_c, gl_c = xs
        u = uv_c - jnp.einsum('bhck,bhkv->bhcv', w_c, S)
        o = jnp.einsum('bhck,bhkv->bhcv', qd_c, S) + jnp.einsum('bhij,bhjv->bhiv', intra_c, u)
        S = jnp.exp(gl_c)[..., None, None] * S + jnp.einsum('bhck,bhcv->bhkv', kt_c, u)
        return S, o

    xs = tuple(jnp.moveaxis(a, 2, 0) for a in (w_mat, u_v, q_dec, intra, k_tail, g_last))
    s_final, o = lax.scan(step, s0, xs)
    o = jnp.transpose(o, (1, 0, 3, 2, 4)).reshape(B, T, H, v.shape[-1])
    return o, s_final


def bidir_delta(q, k, v, g, beta, s0_f, s0_b):
    o_f, s_f = gated_delta_chunked(q, k, v, g[:, :, 0], beta[:, :, 0], s0_f)
    flip = lambda a: jnp.flip(a, axis=1)
    o_b, s_b = gated_delta_chunked(flip(q), flip(k), flip(v), flip(g[:, :, 1]), flip(beta[:, :, 1]), s0_b)
    return o_f + flip(o_b), s_f, s_b


def window_ctx_attention(q, k, v, k_ctx, v_ctx, sink):
    B, S = q.shape[:2]
    L = k_ctx.shape[1]
    nb = S // ATTN_BLOCK
    qb = q.reshape(B, nb, ATTN_BLOCK, ATTN_KV_HEADS, ATTN_GROUP, HEAD_DIM)

    def band(a):
        ap = jnp.pad(a, ((0, 0), (ATTN_BLOCK, ATTN_BLOCK), (0, 0), (0, 0)))
        ap = ap.reshape(B, nb + 2, ATTN_BLOCK, ATTN_KV_HEADS, HEAD_DIM)
        return jnp.concatenate([ap[:, :-2], ap[:, 1:-1], ap[:, 2:]], axis=2)

    kb, vb = band(k), band(v)
    blk = jnp.arange(nb)[:, None, None]
    qpos = blk * ATTN_BLOCK + jnp.arange(ATTN_BLOCK)[None, :, None]
    kpos = (blk - 1) * ATTN_BLOCK + jnp.arange(3 * ATTN_BLOCK)[None, None, :]
    valid = (jnp.abs(qpos - kpos) <= WINDOW) & (kpos >= 0) & (kpos < S)
    scale = HEAD_DIM ** -0.5
    s_loc = jnp.einsum('bnqgrd,bnkgd->bngrqk', qb, kb) * scale
    s_loc = jnp.where(valid[None, :, None, None], s_loc, NEG_INF)
    s_ctx = jnp.einsum('bnqgrd,bkgd->bngrqk', qb, k_ctx) * scale
    s_sink = jnp.broadcast_to(sink.reshape(ATTN_KV_HEADS, ATTN_GROUP)[None, None, :, :, None, None],
                              s_loc.shape[:-1] + (1,))
    p = jax.nn.softmax(jnp.concatenate([s_loc, s_ctx, s_sink], axis=-1).astype(jnp.float32), axis=-1)
    n_loc = 3 * ATTN_BLOCK
    o = (jnp.einsum('bngrqk,bnkgd->bnqgrd', p[..., :n_loc], vb)
         + jnp.einsum('bngrqk,bkgd->bnqgrd', p[..., n_loc:n_loc + L], v_ctx))
    return o.reshape(B, S, ATTN_WIDTH)


def ctx_self_attention(q, k, v, sink):
    B, L = q.shape[:2]
    qg = q.reshape(B, L, ATTN_KV_HEADS, ATTN_GROUP, HEAD_DIM)
    s = jnp.einsum('bqgrd,bkgd->bgrqk', qg, k) * HEAD_DIM ** -0.5
    s_sink = jnp.broadcast_to(sink.reshape(ATTN_KV_HEADS, ATTN_GROUP)[None, :, :, None, None], s.shape[:-1] + (1,))
    p = jax.nn.softmax(jnp.concatenate([s, s_sink], axis=-1).astype(jnp.float32), axis=-1)
    o = jnp.einsum('bgrqk,bkgd->bqgrd', p[..., :L], v)
    return o.reshape(B, L, ATTN_WIDTH)


def trunk_layer(x, ctx, c, c_ctx, w_ada, b_ada, g_norm1, w_in, q_norm_g, k_norm_g, attn_sink, conv_w,
                a_log, dt_bias, dn_norm_g, w_br_attn, w_br_dn, w_out, g_norm2, w_mlp1, w_mlp2, update_ctx):
    B, S = x.shape[:2]
    L = ctx.shape[1]
    rows = S // GRID_W
    row = jnp.broadcast_to(jnp.arange(rows)[:, None], (rows, GRID_W)).reshape(-1)
    col = jnp.broadcast_to(jnp.arange(GRID_W)[None, :], (rows, GRID_W)).reshape(-1)

    mod = jax.nn.silu(c.astype(jnp.float32)) @ w_ada + b_ada
    mod_c = jax.nn.silu(c_ctx.astype(jnp.float32)) @ w_ada + b_ada
    sh1, sc1, gt1, sh2, sc2, gt2 = [m[:, None] for m in jnp.split(mod, 6, axis=-1)]
    csh1, csc1, cgt1, csh2, csc2, cgt2 = jnp.split(mod_c, 6, axis=-1)

    h = rms_norm(x, g_norm1) * (1.0 + sc1) + sh1
    hc = rms_norm(ctx, g_norm1) * (1.0 + csc1) + csh1
    aq, ak, av, dq, dk, dv, dz, da, db, ga, gd = split_cols(h @ w_in)
    caq, cak, cav, cdq, cdk, cdv, cdz, cda, cdb, cga, cgd = split_cols(hc @ w_in)

    def heads(a, n_heads):
        return a.reshape(a.shape[0], a.shape[1], n_heads, HEAD_DIM)

    q_a = axial_rope(rms_norm(heads(aq, ATTN_Q_HEADS), q_norm_g), row, col)
    k_a = axial_rope(rms_norm(heads(ak, ATTN_KV_HEADS), k_norm_g), row, col)
    v_a = heads(av, ATTN_KV_HEADS)
    k_ac = rms_norm(heads(cak, ATTN_KV_HEADS), k_norm_g)
    v_ac = heads(cav, ATTN_KV_HEADS)
    y_attn = window_ctx_attention(q_a, k_a, v_a, k_ac, v_ac, attn_sink)

    def dn_inputs(pq, pk, pv, pa, pb):
        qkv = centred_short_conv(jnp.concatenate([pq, pk, pv], axis=-1), conv_w)
        q_d, k_d, v_d = jnp.split(qkv, 3, axis=-1)
        q_d = l2_normalize(heads(q_d, DN_HEADS)) * DN_HEAD_DIM ** -0.5
        k_d = l2_normalize(heads(k_d, DN_HEADS))
        v_d = heads(v_d, DN_HEADS)
        T = pq.shape[1]
        beta = jax.nn.sigmoid(pb.reshape(pb.shape[0], T, N_DIR, DN_HEADS))
        g = -jnp.exp(a_log) * jax.nn.softplus(pa.reshape(pa.shape[0], T, N_DIR, DN_HEADS) + dt_bias)
        return q_d, k_d, v_d, g, beta

    s_zero = jnp.zeros((B, DN_HEADS, DN_HEAD_DIM, DN_HEAD_DIM), jnp.float32)
    o_dc, s_cf, s_cb = bidir_delta(*dn_inputs(cdq, cdk, cdv, cda, cdb), s_zero, s_zero)
    o_dl, _, _ = bidir_delta(*dn_inputs(dq, dk, dv, da, db), s_cf, s_cb)

    def merge(y_a, o_d, z, gate_a, gate_d):
        y_d = (rms_norm(o_d, dn_norm_g) * jax.nn.silu(heads(z, DN_HEADS))).reshape(z.shape)
        y = jax.nn.sigmoid(gate_a) * (y_a @ w_br_attn) + jax.nn.sigmoid(gate_d) * (y_d @ w_br_dn)
        return y @ w_out

    def mlp(s, shift, scale_m):
        hm = rms_norm(s, g_norm2) * (1.0 + scale_m) + shift
        return jnp.square(jax.nn.relu(hm @ w_mlp1)) @ w_mlp2

    x_new = x.astype(jnp.float32) + gt1 * merge(y_attn, o_dl, dz, ga, gd)
    x_new = x_new + gt2 * mlp(x_new, sh2, sc2)

    if update_ctx:
        q_ac = rms_norm(heads(caq, ATTN_Q_HEADS), q_norm_g)
        y_attn_c = ctx_self_attention(q_ac, k_ac, v_ac, attn_sink)
        ctx = ctx.astype(jnp.float32) + cgt1 * merge(y_attn_c, o_dc, cdz, cga, cgd)
        ctx = ctx + cgt2 * mlp(ctx, csh2, csc2)
    return x_new, ctx


def setup_inputs(seed: int = 0) -> dict:
    key = jax.random.key(seed)
    ks = jax.random.split(key, 24)
    f32 = jnp.float32
    nrm = lambda k, shape, s: jax.random.normal(k, shape, f32) * s
    dt = jnp.exp(jax.random.uniform(ks[12], (DEPTH, N_DIR, DN_HEADS), f32, np.log(1e-3), np.log(1e-1)))
    return {
        "x": nrm(ks[0], (BATCH, SEQ, D_MODEL), 1.0),
        "c": nrm(ks[1], (BATCH, D_MODEL), 1.0),
        "ctx": nrm(ks[2], (BATCH, CTX_LEN, D_MODEL), 1.0),
        "c_ctx": nrm(ks[3], (D_MODEL,), 1.0),
        "w_ada": nrm(ks[4], (DEPTH, D_MODEL, 6 * D_MODEL), 0.5 * D_MODEL ** -0.5),
        "b_ada": nrm(ks[5], (DEPTH, 6 * D_MODEL), 0.02),
        "g_norm1": 1.0 + nrm(ks[6], (DEPTH, D_MODEL), 0.1),
        "w_in": nrm(ks[7], (DEPTH, D_MODEL, IN_WIDTH), D_MODEL ** -0.5),
        "q_norm_g": 1.0 + nrm(ks[8], (DEPTH, HEAD_DIM), 0.1),
        "k_norm_g": 1.0 + nrm(ks[9], (DEPTH, HEAD_DIM), 0.1),
        "attn_sink": nrm(ks[10], (DEPTH, ATTN_Q_HEADS), 0.5),
        "conv_w": nrm(ks[11], (DEPTH, CONV_W, 3 * DN_WIDTH), CONV_W ** -0.5),
        "a_log": jnp.log(jax.random.uniform(ks[13], (DEPTH, N_DIR, DN_HEADS), f32, 1.0, 16.0)),
        "dt_bias": dt + jnp.log(-jnp.expm1(-dt)),
        "dn_norm_g": 1.0 + nrm(ks[14], (DEPTH, DN_HEAD_DIM), 0.1),
        "w_br_attn": nrm(ks[15], (DEPTH, ATTN_WIDTH, D_MODEL), ATTN_WIDTH ** -0.5),
        "w_br_dn": nrm(ks[16], (DEPTH, DN_WIDTH, D_MODEL), DN_WIDTH ** -0.5),
        "w_out": nrm(ks[17], (DEPTH, D_MODEL, D_MODEL), D_MODEL ** -0.5),
        "g_norm2": 1.0 + nrm(ks[18], (DEPTH, D_MODEL), 0.1),
        "w_mlp1": nrm(ks[19], (DEPTH, D_MODEL, D_FF), D_MODEL ** -0.5),
        "w_mlp2": nrm(ks[20], (DEPTH, D_FF, D_MODEL), D_FF ** -0.5),
    }


def reference(x, c, ctx, c_ctx, w_ada, b_ada, g_norm1, w_in, q_norm_g, k_norm_g, attn_sink, conv_w,
              a_log, dt_bias, dn_norm_g, w_br_attn, w_br_dn, w_out, g_norm2, w_mlp1, w_mlp2):
    out_dtype = x.dtype
    for layer in range(DEPTH):
        x, ctx = trunk_layer(x, ctx, c, c_ctx, w_ada[layer], b_ada[layer], g_norm1[layer], w_in[layer],
                             q_norm_g[layer], k_norm_g[layer], attn_sink[layer], conv_w[layer],
                             a_log[layer], dt_bias[layer], dn_norm_g[layer], w_br_attn[layer],
                             w_br_dn[layer], w_out[layer], g_norm2[layer], w_mlp1[layer], w_mlp2[layer],
                             update_ctx=layer < DEPTH - 1)
    return x.astype(out_dtype)
```

```python
import os
import numpy as np
import concourse.bass as bass
import concourse.mybir as mybir
from concourse.bass_utils import run_bass_kernel_spmd

F32 = mybir.dt.float32
BF16 = mybir.dt.bfloat16
U8 = mybir.dt.uint8
ALU = mybir.AluOpType
AF = mybir.ActivationFunctionType
AX = mybir.AxisListType

D = 1024
SEQ = 2048
CTX = 256
NT = 18
TOK = NT * 128
IN_W = 4896
EPS = 1e-6
BIG = 30000.0
ATTACH_WAIT = True


class Res:
    __slots__ = ("name", "w", "r", "dsem", "dcount", "excl")

    def __init__(self, name, excl=False):
        self.name = name
        self.excl = excl
        self.w = None
        self.r = {}
        self.dsem = None
        self.dcount = 0


class Sched:
    ENGS = ("pe", "act", "dve", "pool", "sp")

    def __init__(self, nc):
        self.nc = nc
        self.eng = {"pe": nc.tensor, "act": nc.scalar, "dve": nc.vector,
                    "pool": nc.gpsimd, "sp": nc.sync}
        self.streams = {e: [] for e in self.ENGS}
        self.seen = {e: {} for e in self.ENGS}
        self.last = {e: None for e in self.ENGS}
        self.dtoks = []
        self.esem = {}
        self.nsem = 0
        self.snap = {e: [] for e in self.ENGS}
        self.dsnap = {}

    def _sem(self, name):
        self.nsem += 1
        return self.nc.alloc_semaphore(name=name)

    def _collect(self, eng, r, w):
        deps = []
        for res in r:
            if res.w is not None:
                deps.append((res.w, True))
            if res.excl:
                for k_, t in res.r.items():
                    if k_ != eng:
                        deps.append((t, False))
        for res in w:
            if res.w is not None:
                deps.append((res.w, False))
            for t in res.r.values():
                deps.append((t, False))
        best = {}
        for t, raw in deps:
            if t[0] == "e":
                if t[1] == eng and (eng == "pe" or not raw):
                    continue
                key = ("e", t[1])
            else:
                key = ("d", t[1])
            val = t[2]
            if best.get(key, -1) < val:
                best[key] = val
        waits = []
        seen = self.seen[eng]
        for key, val in sorted(best.items(), key=lambda kv: -kv[1]):
            if seen.get(key, -1) >= val:
                continue
            seen[key] = val
            waits.append((key, val))
            hist = self.snap[key[1]][val] if key[0] == "e" else self.dsnap.get((key[1], val))
            if hist:
                for k2, v2 in hist.items():
                    if k2 != ("e", eng) and seen.get(k2, -1) < v2:
                        seen[k2] = v2
        return waits

    def op(self, eng, fn, r=(), w=()):
        r = [x for x in r if x is not None]
        w = [x for x in w if x is not None]
        waits = self._collect(eng, r, w)
        st = self.streams[eng]
        idx = len(st)
        tok = ("e", eng, idx)
        st.append({"fn": fn, "waits": waits, "inc": False, "dma": None})
        self.snap[eng].append(dict(self.seen[eng]))
        self.last[eng] = tok
        for res in r:
            res.r[eng] = tok
        for res in w:
            res.w = tok
            res.r = {}
        return tok

    def dma(self, eng, pairs, r=(), w=()):
        r = [x for x in r if x is not None]
        w = [x for x in w if x is not None]
        waits = self._collect(eng, r, w)
        key_res = w[0] if w else r[0]
        if key_res.dsem is None:
            key_res.dsem = self._sem("d_" + key_res.name)
        key_res.dcount += 16 * len(pairs)
        dtok = ("d", key_res.dsem, key_res.dcount)
        self.streams[eng].append({"fn": None, "waits": waits, "inc": False,
                                  "dma": (pairs, key_res.dsem)})
        self.snap[eng].append(dict(self.seen[eng]))
        self.dsnap[(key_res.dsem, key_res.dcount)] = dict(self.seen[eng])
        self.dtoks.append(dtok)
        for res in r:
            res.r[("dma", id(key_res))] = dtok
        for res in w:
            res.w = dtok
            res.r = {}
        return dtok

    def wait_only(self, eng, toks):
        waits = []
        for t in toks:
            if t is None:
                continue
            if t[0] == "e":
                if t[1] == eng:
                    continue
                key = ("e", t[1])
            else:
                key = ("d", t[1])
            if self.seen[eng].get(key, -1) >= t[2]:
                continue
            self.seen[eng][key] = t[2]
            waits.append((key, t[2]))
        if waits:
            self.streams[eng].append({"fn": None, "waits": waits, "inc": False, "dma": None})
            self.snap[eng].append(dict(self.seen[eng]))

    def barrier(self):
        toks = [self.last[e] for e in self.ENGS if self.last[e] is not None] + self.dtoks
        for e in self.ENGS:
            self.wait_only(e, toks)
        self.dtoks = []

    def emit(self):
        for e in self.ENGS:
            for o in self.streams[e]:
                for (key, val) in o["waits"]:
                    if key[0] == "e":
                        self.streams[key[1]][val]["inc"] = True
        counts = {}
        for e in self.ENGS:
            c = 0
            cs = []
            for o in self.streams[e]:
                if o["inc"]:
                    c += 1
                cs.append(c)
            counts[e] = cs
            if c > 0:
                self.esem[e] = self._sem("e_" + e)
        ninst = 0
        for e in self.ENGS:
            eo = self.eng[e]
            for o in self.streams[e]:
                waits = []
                for (key, val) in o["waits"]:
                    if key[0] == "e":
                        waits.append((self.esem[key[1]], counts[key[1]][val]))
                    else:
                        waits.append((key[1], val))
                has_ins = (o["dma"] is not None) or (o["fn"] is not None)
                attach = waits.pop() if (has_ins and waits and ATTACH_WAIT) else None
                for (sem, val) in waits:
                    eo.wait_ge(sem, val)
                    ninst += 1
                if o["dma"] is not None:
                    pairs, dsem = o["dma"]
                    for pi, (oa, ia) in enumerate(pairs):
                        ins = eo.dma_start(out=oa, in_=ia)
                        if pi == 0 and attach is not None:
                            ins._wait_ge(attach[0], attach[1])
                        ins.then_inc(dsem, 16)
                        ninst += 1
                elif o["fn"] is not None:
                    ins = o["fn"](eo)
                    if attach is not None:
                        ins._wait_ge(attach[0], attach[1])
                    ninst += 1
                    if o["inc"]:
                        ins.then_inc(self.esem[e], 1)
        return ninst


class Arena:
    def __init__(self, nc, nbytes):
        self.t = nc.alloc_sbuf_tensor("arena", [128, nbytes], U8)
        self.n = nbytes
        self.live = {}

    def alloc(self, name, shape, dtype):
        esz = 4 if dtype == F32 else 2
        nel = 1
        for s in shape[1:]:
            nel *= s
        nb = (nel * esz + 63) // 64 * 64
        segs = sorted(self.live.values())
        off = 0
        for (o, s) in segs:
            if off + nb <= o:
                break
            off = max(off, o + s)
        if off + nb > self.n:
            print("ARENA MAP", sorted((o, sz, n) for n, (o, sz) in self.live.items()))
        assert off + nb <= self.n, ("arena overflow", name, nb, off, self.n)
        assert name not in self.live, name
        self.live[name] = (off, nb)
        ap = self.t[:, off:off + nel * esz].bitcast(dtype)
        if len(shape) == 3:
            ap = ap.rearrange("p (a b) -> p a b", a=shape[1])
        elif len(shape) == 4:
            ap = ap.rearrange("p (a b c) -> p a b c", a=shape[1], b=shape[2])
        return ap

    def free(self, *names):
        for n in names:
            del self.live[n]


def bc(ap, shape):
    return ap.to_broadcast(list(shape))


def _consts():
    p = np.arange(128)
    same = (p[:, None] // 64) == (p[None, :] // 64)
    c = {}
    c["ident"] = np.eye(128, dtype=np.float32)
    c["ones"] = np.ones((128, 128), np.float32)
    c["blk"] = same.astype(np.float32)
    c["tri_f"] = (same & (p[:, None] <= p[None, :])).astype(np.float32)
    c["tri_b"] = (same & (p[:, None] >= p[None, :])).astype(np.float32)
    inc_f = same & (p[None, :] <= p[:, None])
    inc_b = same & (p[None, :] >= p[:, None])
    c["big_f"] = np.where(inc_f, 0.0, BIG).astype(np.float32)
    c["big_b"] = np.where(inc_b, 0.0, BIG).astype(np.float32)
    c["strict_f"] = (same & (p[None, :] < p[:, None])).astype(np.float32)
    c["strict_b"] = (same & (p[None, :] > p[:, None])).astype(np.float32)
    c["mask_p"] = (p[:, None] >= p[None, :]).astype(np.float32)
    c["mask_n"] = (p[:, None] <= p[None, :]).astype(np.float32)
    sm = lambda b: (p[:, None] // b) == (p[None, :] // b)
    c["m8"] = sm(8).astype(np.float32)
    c["l16"] = (sm(16) & ~sm(8)).astype(np.float32)
    c["l32"] = (sm(32) & ~sm(16)).astype(np.float32)
    c["l64"] = (sm(64) & ~sm(32)).astype(np.float32)
    names = ["ident", "ones", "blk", "tri_f", "tri_b", "big_f", "big_b",
             "strict_f", "strict_b", "mask_p", "mask_n", "m8", "l16", "l32", "l64"]
    arr = np.concatenate([c[n] for n in names], axis=1)
    return names, np.ascontiguousarray(arr)


def _rope_tables():
    t = np.arange(SEQ)
    row = (t // 64).astype(np.float32)
    col = (t % 64).astype(np.float32)
    freqs = (10000.0 ** (-np.arange(16, dtype=np.float32) / 16)).astype(np.float32)
    ar = row[:, None] * freqs[None, :]
    ac = col[:, None] * freqs[None, :]
    cos = np.concatenate([np.cos(ar), np.cos(ar), np.cos(ac), np.cos(ac)], axis=1)
    sin = np.concatenate([-np.sin(ar), np.sin(ar), -np.sin(ac), np.sin(ac)], axis=1)
    return cos.astype(np.float32), sin.astype(np.float32)


CONST_NAMES, CONST_ARR = _consts()
ROPE_COS, ROPE_SIN = _rope_tables()

DBG = {}


def build(debug=(), stop_after=None):
    nc = bass.Bass("TRN2", target_bir_lowering=False)
    S = Sched(nc)
    A = Arena(nc, 206 * 1024)

    def din(name, shape):
        return nc.dram_tensor(name, list(shape), F32, kind="ExternalInput").ap()

    x_d = din("x", [SEQ, D])
    ctx_d = din("ctx", [CTX, D])
    cT_d = din("cT", [128, 16])
    badaT_d = din("badaT", [128, 48])
    bada_d = din("bada", [6 * D])
    g1T_d = din("g1T", [128, 8])
    g2T_d = din("g2T", [128, 8])
    wada_d = din("w_ada", [D, 6 * D])
    win_d = din("w_in", [D, IN_W])
    qg_d = din("q_norm_g", [64])
    kg_d = din("k_norm_g", [64])
    sink_d = din("attn_sink", [8])
    convT_d = din("convT", [128, 36])
    alog_d = din("a_log", [16])
    dtb_d = din("dt_bias", [16])
    dng_d = din("dn_norm_g", [64])
    wbra_d = din("w_br_attn", [512, D])
    wbrd_d = din("w_br_dn", [512, D])
    wout_d = din("w_out", [D, D])
    w1_d = din("w_mlp1", [D, 4 * D])
    w2_d = din("w_mlp2", [4 * D, D])
    const_d = din("consts", list(CONST_ARR.shape))
    cos_d = din("rope_cos", [SEQ, 64])
    sin_d = din("rope_sin", [SEQ, 64])
    out_d = nc.dram_tensor("out", [SEQ, D], F32, kind="ExternalOutput").ap()

    dbg_out = {}

    def dump(name, ap, res):
        if name not in debug:
            return
        shape = list(ap.shape)
        d = nc.dram_tensor("dbg_" + name, shape, F32, kind="ExternalOutput").ap()
        dbg_out[name] = shape
        S.dma("pool", [(d, ap)], r=[res])

    banks = []
    for i in range(8):
        t = nc.alloc_psum_tensor("pb%d" % i, [128, 512], F32)
        banks.append((t[:], t[:].bitcast(BF16), Res("pb%d" % i, excl=True)))
    bank_i = [0]

    def nb():
        b = banks[bank_i[0] % 8]
        bank_i[0] += 1
        return b

    def mm(out, lhsT, rhs, start=True, stop=True, r=(), w=(), skip=False):
        if skip:
            S.op("pe", lambda e: e.matmul(out, lhsT, rhs, start=start, stop=stop, skip_group_check=True), r=r, w=w)
        else:
            S.op("pe", lambda e: e.matmul(out, lhsT, rhs, start=start, stop=stop), r=r, w=w)

    def tr(out, in_, ident, r=(), w=()):
        S.op("pe", lambda e: e.transpose(out, in_, ident), r=r, w=w)

    def act(out, in_, func, r=(), w=(), scale=1.0, bias=None, accum=None):
        def f(e):
            kw = {"scale": scale}
            if bias is not None:
                kw["bias"] = bias
            if accum is not None:
                kw["accum_out"] = accum
            return e.activation(out, in_, func, **kw)
        S.op("act", f, r=r, w=w)

    def tt(eng, out, in0, in1, op, r=(), w=()):
        S.op(eng, lambda e: e.tensor_tensor(out, in0, in1, op), r=r, w=w)

    def ts(eng, out, in0, s1, op0, s2=None, op1=None, r=(), w=()):
        if op1 is None:
            S.op(eng, lambda e: e.tensor_scalar(out, in0, s1, None, op0), r=r, w=w)
        else:
            S.op(eng, lambda e: e.tensor_scalar(out, in0, s1, s2, op0, op1), r=r, w=w)

    def stt(eng, out, in0, scalar, in1, op0, op1, r=(), w=()):
        S.op(eng, lambda e: e.scalar_tensor_tensor(out, in0, scalar, in1, op0, op1), r=r, w=w)

    def cp(eng, out, in_, r=(), w=()):
        if eng == "act":
            S.op("act", lambda e: e.activation(out, in_, AF.Copy), r=r, w=w)
        else:
            S.op(eng, lambda e: e.tensor_copy(out, in_), r=r, w=w)

    def recip(out, in_, r=(), w=()):
        S.op("dve", lambda e: e.reciprocal(out, in_), r=r, w=w)

    def memset(eng, ap, val, w=()):
        S.op(eng, lambda e: e.memset(ap, val), w=w)

    def rstd_from_ss(out, ss, inv_n, tmp, r_ss, r_tmp, r_out):
        ts("dve", tmp, ss, inv_n, ALU.mult, EPS, ALU.add, r=[r_ss], w=[r_tmp])
        act(tmp, tmp, AF.Ln, r=[r_tmp], w=[r_tmp])
        act(out, tmp, AF.Exp, scale=-0.5, r=[r_tmp], w=[r_out])

    NC_ = CONST_ARR.shape[1]
    cidx = {n: i for i, n in enumerate(CONST_NAMES)}

    cb = A.alloc("constb", [128, NC_], BF16)
    r_cb = Res("constb")
    S.dma("pool", [(cb, const_d)], w=[r_cb])

    def cB(name):
        i = cidx[name]
        return cb[:, i * 128:(i + 1) * 128]

    small = A.alloc("small", [128, 512], F32)
    r_small = Res("small")
    so = [0]

    def salloc(n):
        o = so[0]
        so[0] += n
        assert so[0] <= 512
        return small[:, o:o + n]

    cT = salloc(16)
    badaT = salloc(48)
    g1T = salloc(8)
    g2T = salloc(8)
    convT = salloc(36)
    qg_b = salloc(64)
    kg_b = salloc(64)
    dng_b = salloc(64)
    sink_b = salloc(8)
    alog_b = salloc(16)
    dtb_b = salloc(16)
    S.dma("sp", [(cT, cT_d), (badaT, badaT_d), (g1T, g1T_d), (g2T, g2T_d), (convT, convT_d),
                 (qg_b, qg_d.partition_broadcast(128)), (kg_b, kg_d.partition_broadcast(128)),
                 (dng_b, dng_d.partition_broadcast(128)), (sink_b, sink_d.partition_broadcast(128)),
                 (alog_b, alog_d.partition_broadcast(128)), (dtb_b, dtb_d.partition_broadcast(128))],
          w=[r_small])
    esink = salloc(8)
    negA = salloc(16)
    act(esink, sink_b, AF.Exp, r=[r_small], w=[r_small])
    act(negA, alog_b, AF.Exp, r=[r_small], w=[r_small])
    ts("dve", negA, negA, -1.0, ALU.mult, r=[r_small], w=[r_small])

    gtrow = A.alloc("gtrow", [128, 2, D], F32)
    r_gtrow = Res("gtrow")
    S.dma("sp", [(gtrow[:, 0, :], bada_d[2 * D:3 * D].partition_broadcast(128)),
                 (gtrow[:, 1, :], bada_d[5 * D:6 * D].partition_broadcast(128))], w=[r_gtrow])

    modv = A.alloc("modv", [128, 48, 2], F32)
    r_modv = Res("modv")
    coef = A.alloc("coef", [128, 4, 8], F32)
    r_coef = Res("coef")

    scT = A.alloc("scT", [128, 16], BF16)
    screp = A.alloc("screp", [128, 8, 128], BF16)
    r_sc = Res("sc")
    tmp16 = A.alloc("tmp16", [128, 16], F32)
    r_tmp16 = Res("tmp16")
    act(tmp16, cT, AF.Exp, scale=-1.0, r=[r_small], w=[r_tmp16])
    ts("dve", tmp16, tmp16, 1.0, ALU.add, r=[r_tmp16], w=[r_tmp16])
    recip(tmp16, tmp16, r=[r_tmp16], w=[r_tmp16])
    tt("dve", scT, cT, tmp16, ALU.mult, r=[r_small, r_tmp16], w=[r_sc])
    cp("dve", screp, bc(scT.rearrange("p (k t) -> p k t", t=2)[:, :, 0:1], [128, 8, 128]), r=[r_sc], w=[r_sc])

    wada_v = wada_d.rearrange("(k p) n -> p k n", p=128)
    wab = [A.alloc("wada%d" % i, [128, 8, 512], BF16) for i in range(2)]
    r_wab = [Res("wada%d" % i) for i in range(2)]
    psm, _, r_psm = nb()
    for j in range(12):
        wa, r_wa = wab[j % 2], r_wab[j % 2]
        S.dma("pool", [(wa, wada_v[:, :, j * 512:(j + 1) * 512])], w=[r_wa])
        for m in range(4):
            cidx_ = j * 4 + m
            for k in range(8):
                mm(psm[:, cidx_ * 2:cidx_ * 2 + 2], wa[:, k, m * 128:(m + 1) * 128], scT[:, 2 * k:2 * k + 2],
                   start=(k == 0), stop=(k == 7), r=[r_wa, r_sc], w=[r_psm])
        if j in (4, 5, 10, 11):
            which = 0 if j < 6 else 1
            half = j % 2
            psr, _, r_psr = nb()
            for k in range(8):
                mm(psr[:, 0:512], screp[:, k, :], wa[:, k, :], start=(k == 0), stop=(k == 7),
                   r=[r_wa, r_sc], w=[r_psr])
            dst = gtrow[:, which, half * 512:(half + 1) * 512]
            tt("dve", dst, psr[:, 0:512], dst, ALU.add, r=[r_psr, r_gtrow], w=[r_gtrow])
    tt("dve", modv, psm[:, 0:96].rearrange("p (c t) -> p c t", t=2), bc(badaT.unsqueeze(2), [128, 48, 2]),
       ALU.add, r=[r_psm, r_small], w=[r_modv])
    stt("dve", coef[:, 0, :], modv[:, 8:16, 0], 1.0, g1T, ALU.add, ALU.mult, r=[r_modv, r_small], w=[r_coef])
    stt("dve", coef[:, 1, :], modv[:, 8:16, 1], 1.0, g1T, ALU.add, ALU.mult, r=[r_modv, r_small], w=[r_coef])
    stt("dve", coef[:, 2, :], modv[:, 32:40, 0], 1.0, g2T, ALU.add, ALU.mult, r=[r_modv, r_small], w=[r_coef])
    dump("modv", modv, r_modv)
    dump("gtrow", gtrow, r_gtrow)
    A.free("wada0", "wada1", "scT", "screp", "tmp16")

    hT = A.alloc("hT", [128, 8, TOK], BF16)
    r_hT = [Res("hT%d" % t) for t in range(NT)]
    stat = A.alloc("stat", [128, 64], F32)
    r_stat = Res("stat")

    def nt_front(src_ap, r_src, bufs, i):
        junk, r_junk, xn, r_xn = bufs
        ss = stat[:, (i % 8) * 4:(i % 8) * 4 + 1]
        tm = stat[:, (i % 8) * 4 + 1:(i % 8) * 4 + 2]
        rs = stat[:, (i % 8) * 4 + 2:(i % 8) * 4 + 3]
        memset("dve", ss, 0.0, w=[r_stat])
        act(junk, src_ap, AF.Square, accum=ss, r=[r_src, r_stat], w=[r_junk, r_stat])
        rstd_from_ss(rs, ss, 1.0 / D, tm, r_stat, r_stat, r_stat)
        ts("dve", xn, src_ap, rs, ALU.mult, r=[r_src, r_stat], w=[r_xn])

    def nt_back(dstT, tokslice, r_dst, Acoef, Bcoef, r_ab, bufs, i):
        junk, r_junk, xn, r_xn = bufs
        _, pbb, r_pb = nb()
        for c in range(8):
            tr(pbb[:, c * 128:(c + 1) * 128], xn[:, c * 128:(c + 1) * 128], cB("ident"),
               r=[r_xn, r_cb], w=[r_pb])
        for c in range(8):
            dst = dstT[:, c, tokslice]
            if i % 2 == 0:
                act(dst, pbb[:, c * 128:(c + 1) * 128], AF.Identity, scale=Acoef[:, c:c + 1], bias=Bcoef(c),
                    r=[r_pb] + r_ab, w=[r_dst])
            else:
                ts("dve", dst, pbb[:, c * 128:(c + 1) * 128], Acoef[:, c:c + 1], ALU.mult, Bcoef(c), ALU.add,
                   r=[r_pb] + r_ab, w=[r_dst])

    def norm_transpose_seq(items):
        for k, it in enumerate(items):
            if k == 0:
                if it.get("pre"):
                    it["pre"]()
                nt_front(it["src"], it["r_src"], it["bufs"], it["i"])
            if k + 1 < len(items):
                nx = items[k + 1]
                if nx.get("pre"):
                    nx["pre"]()
                nt_front(nx["src"], nx["r_src"], nx["bufs"], nx["i"])
            nt_back(it["dstT"], it["tok"], it["r_dst"], it["A"], it["B"], [r_coef, r_modv], it["bufs"], it["i"])

    xbuf = [A.alloc("xbuf%d" % i, [128, D], F32) for i in range(2)]
    r_xbuf = [Res("xbuf%d" % i) for i in range(2)]
    junk = A.alloc("junk", [128, D], BF16)
    r_junk = Res("junk")
    xnb = [A.alloc("xn%d" % i, [128, D], BF16) for i in range(2)]
    r_xnb = [Res("xn%d" % i) for i in range(2)]
    items = []
    for t in range(NT):
        src = ctx_d[t * 128:(t + 1) * 128, :] if t < 2 else x_d[(t - 2) * 128:(t - 1) * 128, :]
        xb, r_xb = xbuf[t % 2], r_xbuf[t % 2]
        if t < 2:
            Ac = coef[:, 1, :]
            Bc = (lambda c: modv[:, c, 1:2])
        else:
            Ac = coef[:, 0, :]
            Bc = (lambda c: modv[:, c, 0:1])
        items.append(dict(src=xb, r_src=r_xb, dstT=hT, tok=slice(t * 128, (t + 1) * 128), r_dst=r_hT[t], A=Ac, B=Bc,
                          bufs=(junk, r_junk, xnb[t % 2], r_xnb[t % 2]), i=t,
                          pre=(lambda xb=xb, src=src, r_xb=r_xb: S.dma("sp", [(xb, src)], w=[r_xb]))))
    norm_transpose_seq(items)
    if "hT" in debug:
        r_all = Res("hTall")
        S.wait_only("pool", [r.w for r in r_hT])
        dump("hT", hT, r_all)
    A.free("xbuf0", "xbuf1", "junk", "xn0", "xn1")
    if stop_after == 1:
        return finish(nc, S, out_d, dbg_out, None)
    S.barrier()

    win_v = win_d.rearrange("(k p) n -> p k n", p=128)
    wq = A.alloc("wq", [128, 8, 512], BF16)
    wkv = A.alloc("wkv", [128, 8, 288], BF16)
    r_wq, r_wkv = Res("wq"), Res("wkv")
    S.dma("pool", [(wq, win_v[:, :, 0:512])], w=[r_wq])
    S.dma("pool", [(wkv[:, :, 0:256], win_v[:, :, 512:768]), (wkv[:, :, 256:288], win_v[:, :, 2816:2848])], w=[r_wkv])
    cosT = A.alloc("cosT", [128, 16, 64], F32)
    sinT = A.alloc("sinT", [128, 16, 64], F32)
    r_rope = Res("rope")
    S.dma("sp", [(cosT, cos_d.rearrange("(t p) d -> p t d", p=128)),
                 (sinT, sin_d.rearrange("(t p) d -> p t d", p=128))], w=[r_rope])
    qT = A.alloc("qT", [128, 8, SEQ], BF16)
    kT = A.alloc("kT", [128, 2, TOK], BF16)
    vA = A.alloc("vA", [128, NT, 2, 65], BF16)
    r_qT = [Res("qT%d" % t) for t in range(16)]
    r_kT = [Res("kT%d" % t) for t in range(NT)]
    r_vA = [Res("vA%d" % t) for t in range(NT)]
    gall = A.alloc("gall", [128, NT, 16], F32)
    ball = A.alloc("ball", [128, NT, 16], F32)
    r_g = [Res("g%d" % t) for t in range(NT)]
    memset("pool", vA[:, :, :, 64:65], 1.0, w=r_vA)

    wk2 = [{}, {}]
    for nm, shp, dt in [("qsq", [128, 512], F32), ("qn", [128, 8, 64], F32), ("qg", [128, 8, 64], F32),
                        ("qt", [128, 8, 64], F32), ("qu", [128, 8, 64], F32), ("qr", [128, 512], BF16),
                        ("ksq", [128, 128], F32), ("kn", [128, 2, 64], F32), ("kg", [128, 2, 64], F32),
                        ("kt", [128, 2, 64], F32), ("ku", [128, 2, 64], F32), ("kr", [128, 128], BF16),
                        ("gt", [128, 64], F32)]:
        for pp in range(2):
            wk2[pp][nm] = (A.alloc("w%d_%s" % (pp, nm), shp, dt), Res("w%d_%s" % (pp, nm)))
    wk = wk2[0]

    def qk_norm_rope(ps_ap, r_ps, nh, gain_b, tile_lat, pre, wk):
        sq, r_sq = wk[pre + "sq"]
        xn, r_xn = wk[pre + "n"]
        xg, r_xg = wk[pre + "g"]
        xt_, r_xt = wk[pre + "t"]
        xu, r_xu = wk[pre + "u"]
        xr, r_xr = wk[pre + "r"]
        gt_, r_gt = wk["gt"]
        W = nh * 64
        ps3 = ps_ap.rearrange("p (h d) -> p h d", h=nh)
        act(sq[:, 0:W], ps_ap, AF.Square, r=[r_ps], w=[r_sq])
        ss = gt_[:, 0:nh]
        tm = gt_[:, 8:8 + nh]
        rs = gt_[:, 16:16 + nh]
        S.op("dve", lambda e: e.tensor_reduce(ss, sq[:, 0:W].rearrange("p (h d) -> p h d", h=nh), AX.X, ALU.add),
             r=[r_sq], w=[r_gt])
        rstd_from_ss(rs, ss, 1.0 / 64, tm, r_gt, r_gt, r_gt)
        tt("dve", xn, ps3, bc(rs.unsqueeze(2), [128, nh, 64]), ALU.mult, r=[r_ps, r_gt], w=[r_xn])
        if tile_lat is None:
            tt("pool", xr.rearrange("p (h d) -> p h d", h=nh), xn, bc(gain_b.unsqueeze(1), [128, nh, 64]), ALU.mult,
               r=[r_xn, r_small], w=[r_xr])
            return xr, r_xr
        tt("pool", xg, xn, bc(gain_b.unsqueeze(1), [128, nh, 64]), ALU.mult, r=[r_xn, r_small], w=[r_xg])
        cs = cosT[:, tile_lat, :]
        sn = sinT[:, tile_lat, :]
        tt("pool", xt_, xg, bc(cs.unsqueeze(1), [128, nh, 64]), ALU.mult, r=[r_xg, r_rope], w=[r_xt])
        xg5 = xg.rearrange("p h (a b c) -> p h a b c", a=2, b=2)
        xu5 = xu.rearrange("p h (a b c) -> p h a b c", a=2, b=2)
        sn4 = sn.rearrange("p (a b c) -> p a b c", a=2, b=2)
        for bsel in range(2):
            tt("dve", xu5[:, :, :, bsel, :], xg5[:, :, :, 1 - bsel, :],
               bc(sn4[:, :, bsel, :].unsqueeze(1), [128, nh, 2, 16]), ALU.mult, r=[r_xg, r_rope], w=[r_xu])
        tt("pool", xr.rearrange("p (h d) -> p h d", h=nh), xt_, xu, ALU.add, r=[r_xt, r_xu], w=[r_xr])
        return xr, r_xr

    tstate = {}

    def p2_proj(t):
        lat = t - 2 if t >= 2 else None
        psq, _, r_psq = nb()
        pskv, _, r_pskv = nb()
        if lat is not None:
            for k in range(8):
                mm(psq[:, 0:512], hT[:, k, t * 128:(t + 1) * 128], wq[:, k, :], start=(k == 0), stop=(k == 7),
                   r=[r_hT[t], r_wq], w=[r_psq])
        for k in range(8):
            mm(pskv[:, 0:288], hT[:, k, t * 128:(t + 1) * 128], wkv[:, k, :], start=(k == 0), stop=(k == 7),
               r=[r_hT[t], r_wkv], w=[r_pskv])
        tstate[t] = {"psq": (psq, r_psq), "pskv": (pskv, r_pskv)}

    def p2_mid(t):
        lat = t - 2 if t >= 2 else None
        wk = wk2[t % 2]
        psq, r_psq = tstate[t]["psq"]
        pskv, r_pskv = tstate[t]["pskv"]
        cp("act", vA[:, t, :, 0:64], pskv[:, 128:256].rearrange("p (h d) -> p h d", h=2), r=[r_pskv], w=[r_vA[t]])
        gt_, r_gt = wk["gt"]
        xa = gt_[:, 24:40]
        xb_ = gt_[:, 40:56]
        tt("dve", xa, pskv[:, 256:272], dtb_b, ALU.add, r=[r_pskv, r_small], w=[r_gt])
        act(xa, xa, AF.Exp, r=[r_gt], w=[r_gt])
        ts("dve", xa, xa, 1.0, ALU.add, r=[r_gt], w=[r_gt])
        act(xa, xa, AF.Ln, r=[r_gt], w=[r_gt])
        tt("dve", gall[:, t, :], xa, negA, ALU.mult, r=[r_gt, r_small], w=[r_g[t]])
        act(xb_, pskv[:, 272:288], AF.Exp, scale=-1.0, r=[r_pskv], w=[r_gt])
        ts("dve", xb_, xb_, 1.0, ALU.add, r=[r_gt], w=[r_gt])
        recip(ball[:, t, :], xb_, r=[r_gt], w=[r_g[t]])
        tstate[t]["kr"] = qk_norm_rope(pskv[:, 0:128], r_pskv, 2, kg_b, lat, "k", wk)
        if lat is not None:
            tstate[t]["qr"] = qk_norm_rope(psq[:, 0:512], r_psq, 8, qg_b, lat, "q", wk)

    def p2_back(t):
        lat = t - 2 if t >= 2 else None
        kr, r_kr = tstate[t]["kr"]
        _, pbb, r_pb = nb()
        for g in range(2):
            tr(pbb[0:64, g * 128:(g + 1) * 128], kr[:, g * 64:(g + 1) * 64], cB("ident"), r=[r_kr, r_cb], w=[r_pb])
        cp("dve", kT[0:64, :, t * 128:(t + 1) * 128], pbb[0:64, 0:256].rearrange("p (g t) -> p g t", g=2),
           r=[r_pb], w=[r_kT[t]])
        if lat is not None:
            qr, r_qr = tstate[t]["qr"]
            _, pbb2, r_pb2 = nb()
            for h in range(8):
                tr(pbb2[0:64, h * 128:(h + 1) * 128], qr[:, h * 64:(h + 1) * 64], cB("ident"), r=[r_qr, r_cb], w=[r_pb2])
            cp("act", qT[0:64, :, lat * 128:(lat + 1) * 128], pbb2[0:64, :].rearrange("p (h t) -> p h t", h=8),
               r=[r_pb2], w=[r_qT[lat]])

    p2_proj(0)
    p2_proj(1)
    p2_mid(0)
    for t in range(NT):
        if t + 2 < NT:
            p2_proj(t + 2)
        if t + 1 < NT:
            p2_mid(t + 1)
        p2_back(t)
    gp = [A.alloc("gp%d" % i, [128, NT, 16], BF16) for i in range(3)]
    r_gp = Res("gp")
    gr = [A.alloc("gr%d" % i, [128, NT, 16], F32) for i in range(2)]
    r_gr = Res("gr")
    cp("dve", gp[0], gall, r=r_g, w=[r_gp])
    tt("dve", gr[0], gall, gp[0], ALU.subtract, r=r_g + [r_gp], w=[r_gr])
    cp("dve", gp[1], gr[0], r=[r_gr], w=[r_gp])
    tt("dve", gr[1], gr[0], gp[1], ALU.subtract, r=[r_gr, r_gp], w=[r_gr])
    cp("dve", gp[2], gr[1], r=[r_gr], w=[r_gp])
    A.free("gr0", "gr1")
    if "qT" in debug:
        r_all = Res("dall")
        S.wait_only("pool", [r.w for r in r_qT] + [r.w for r in r_kT] + [r.w for r in r_vA] + [r.w for r in r_g])
        dump("qT", qT[0:64], r_all)
        dump("kT", kT[0:64], r_all)
        dump("vA", vA, r_all)
        dump("gall", gall, r_all)
        dump("ball", ball, r_all)
    for pp in range(2):
        for nm in list(wk2[pp].keys()):
            A.free("w%d_%s" % (pp, nm))
    A.free("wq", "wkv", "cosT", "sinT")
    if stop_after == 2:
        return finish(nc, S, out_d, dbg_out, None)
    S.barrier()

    yattnT = A.alloc("yattnT", [128, 4, SEQ], BF16)
    r_yat = [Res("yat%d" % t) for t in range(16)]
    ptb = [A.alloc("pt%d" % i, [128, 512], BF16) for i in range(3)]
    r_ptb = [Res("pt%d" % i) for i in range(3)]
    ytile = [A.alloc("ytile%d" % i, [128, 512], BF16) for i in range(2)]
    r_ytile = [Res("ytile%d" % i) for i in range(2)]
    den = A.alloc("den", [128, 16], F32)
    r_den = Res("den")
    obanks = [banks[0], banks[1]]
    sbanks = [banks[2], banks[3], banks[4]]
    tbanks = [banks[5], banks[6]]
    its = []
    for n in range(16):
        for g in range(2):
            kbs = []
            if n > 0:
                kbs.append((n + 1, "mask_p"))
            kbs.append((n + 2, None))
            if n < 15:
                kbs.append((n + 3, "mask_n"))
            kbs.append((0, None))
            kbs.append((1, None))
            for idx, (kt_i, mk) in enumerate(kbs):
                its.append((n, g, idx, len(kbs), kt_i, mk))

    def emit_qk(ii):
        n, g, idx, nk, kt_i, mk = its[ii]
        pss, _, r_pss = sbanks[ii % 3]
        mm(pss[:, 0:512], kT[0:64, g, kt_i * 128:(kt_i + 1) * 128],
           qT[0:64, 4 * g:4 * g + 4, n * 128:(n + 1) * 128],
           r=[r_kT[kt_i], r_qT[n]], w=[r_pss])

    emit_qk(0)
    emit_qk(1)
    pend_tr = []
    for ii, (n, g, idx, nk, kt_i, mk) in enumerate(its):
        yt_, r_yt = ytile[n % 2], r_ytile[n % 2]
        pso, _, r_pso = obanks[g]
        pss, _, r_pss = sbanks[ii % 3]
        if ii + 2 < len(its):
            emit_qk(ii + 2)
        pt, r_pt = ptb[ii % 3], r_ptb[ii % 3]
        act(pt, pss[:, 0:512], AF.Exp, scale=0.125, r=[r_pss], w=[r_pt])
        if mk is not None:
            tt("dve", pt.rearrange("p (h q) -> p h q", h=4), pt.rearrange("p (h q) -> p h q", h=4),
               bc(cB(mk).unsqueeze(1), [128, 4, 128]), ALU.mult, r=[r_pt, r_cb], w=[r_pt])
        for rr in range(4):
            mm(pso[:, rr * 65:(rr + 1) * 65], pt[:, rr * 128:(rr + 1) * 128], vA[:, kt_i, g, :],
               start=(idx == 0 and rr == 0), stop=(idx == nk - 1), r=[r_pt, r_vA[kt_i]], w=[r_pso],
               skip=True)
        if idx == nk - 1:
            pso3 = pso[:, 0:260].rearrange("p (h d) -> p h d", h=4)
            dn_ = den[:, g * 4:(g + 1) * 4]
            tt("dve", dn_.unsqueeze(2), pso3[:, :, 64:65], esink[:, 4 * g:4 * g + 4].unsqueeze(2), ALU.add,
               r=[r_pso, r_small], w=[r_den])
            recip(dn_, dn_, r=[r_den], w=[r_den])
            tt("dve", yt_[:, g * 256:(g + 1) * 256].rearrange("p (h d) -> p h d", h=4), pso3[:, :, 0:64],
               bc(dn_.unsqueeze(2), [128, 4, 64]), ALU.mult, r=[r_pso, r_den], w=[r_yt])
            if g == 1:
                pend_tr.append((ii + 3, n))
        while pend_tr and (pend_tr[0][0] <= ii or ii == len(its) - 1):
            _, n_ = pend_tr.pop(0)
            ytn, r_ytn = ytile[n_ % 2], r_ytile[n_ % 2]
            _, pbb, r_pb = tbanks[n_ % 2]
            for c in range(4):
                tr(pbb[:, c * 128:(c + 1) * 128], ytn[:, c * 128:(c + 1) * 128], cB("ident"), r=[r_ytn, r_cb], w=[r_pb])
            cp("act", yattnT[:, :, n_ * 128:(n_ + 1) * 128], pbb[:, 0:512].rearrange("p (c t) -> p c t", c=4),
               r=[r_pb], w=[r_yat[n_]])
    if "yattnT" in debug:
        r_all = Res("dall2")
        S.wait_only("pool", [r.w for r in r_yat])
        dump("yattnT", yattnT, r_all)
    A.free("pt0", "pt1", "pt2", "ytile0", "ytile1", "den", "qT", "kT", "vA")
    if stop_after == 3:
        return finish(nc, S, out_d, dbg_out, None)
    S.barrier()

    pre = A.alloc("pre", [128, 12, TOK], BF16)
    r_pre = [Res("pre%d" % c) for c in range(12)]
    wd = [A.alloc("wd%d" % i, [128, 8, 512], BF16) for i in range(2)]
    r_wd = [Res("wd%d" % i) for i in range(2)]
    accb = [A.alloc("acc%d" % i, [128, TOK], F32) for i in range(2)]
    r_accb = [Res("acc%d" % i) for i in range(2)]
    seqs = [(0, 256), (256, TOK)]

    def conv_silu(c):
        acc, r_acc = accb[c % 2], r_accb[c % 2]
        w0 = convT[:, c * 3 + 0:c * 3 + 1]
        w1c = convT[:, c * 3 + 1:c * 3 + 2]
        w2c = convT[:, c * 3 + 2:c * 3 + 3]
        ts("dve", acc, pre[:, c, :], w1c, ALU.mult, r=[r_pre[c], r_small], w=[r_acc])
        for (s0, e0) in seqs:
            stt("dve", acc[:, s0 + 1:e0], pre[:, c, s0:e0 - 1], w0, acc[:, s0 + 1:e0], ALU.mult, ALU.add,
                r=[r_pre[c], r_small, r_acc], w=[r_acc])
        for (s0, e0) in seqs:
            stt("dve", acc[:, s0:e0 - 1], pre[:, c, s0 + 1:e0], w2c, acc[:, s0:e0 - 1], ALU.mult, ALU.add,
                r=[r_pre[c], r_small, r_acc], w=[r_acc])
        act(pre[:, c, :], acc, AF.Silu, r=[r_acc], w=[r_pre[c]])

    tokblocks = [(0, 256)] + [(256 + 512 * i, 512) for i in range(4)]
    ev = 0
    for j in range(3):
        wdj, r_wdj = wd[j % 2], r_wd[j % 2]
        S.dma("pool", [(wdj, win_v[:, :, 768 + 512 * j:768 + 512 * (j + 1)])], w=[r_wdj])
        for (t0, n) in tokblocks:
            rts = [r_hT[t] for t in range(t0 // 128, (t0 + n) // 128)]
            for cc in range(4):
                c = 4 * j + cc
                ps, _, r_ps = nb()
                for k in range(8):
                    mm(ps[:, 0:n], wdj[:, k, cc * 128:(cc + 1) * 128], hT[:, k, t0:t0 + n], start=(k == 0), stop=(k == 7),
                       r=[r_wdj] + rts, w=[r_ps])
                cp("act", pre[:, c, t0:t0 + n], ps[:, 0:n], r=[r_ps], w=[r_pre[c]])
                ev += 1
        if j > 0:
            for c in range(4 * (j - 1), 4 * j):
                conv_silu(c)
    for c in range(8, 12):
        conv_silu(c)
    A.free("wd0", "wd1")
    A.free("acc0", "acc1")
    sqb = [A.alloc("sqb%d" % i, [128, TOK], BF16) for i in range(2)]
    r_sqb = [Res("sqb%d" % i) for i in range(2)]
    rnb = [A.alloc("rnb%d" % i, [128, TOK], F32) for i in range(2)]
    r_rnb = [Res("rnb%d" % i) for i in range(2)]
    for c in range(8):
        sq_, r_sq_ = sqb[c % 2], r_sqb[c % 2]
        rn, r_rn = rnb[c % 2], r_rnb[c % 2]
        act(sq_, pre[:, c, :], AF.Square, r=[r_pre[c]], w=[r_sq_])
        for (t0, n) in tokblocks:
            ps, _, r_ps = nb()
            mm(ps[:, 0:n], cB("blk"), sq_[:, t0:t0 + n], r=[r_cb, r_sq_], w=[r_ps])
            ts("dve", rn[:, t0:t0 + n], ps[:, 0:n], EPS, ALU.add, 64.0 if c < 4 else 1.0, ALU.mult, r=[r_ps], w=[r_rn])
        act(rn, rn, AF.Ln, r=[r_rn], w=[r_rn])
        act(rn, rn, AF.Exp, scale=-0.5, r=[r_rn], w=[r_rn])
        tt("pool", pre[:, c, :], pre[:, c, :], rn, ALU.mult, r=[r_pre[c], r_rn], w=[r_pre[c]])
    if "pre" in debug:
        r_all = Res("dall3")
        S.wait_only("pool", [r.w for r in r_pre])
        dump("pre", pre, r_all)
    A.free("sqb0", "sqb1", "rnb0", "rnb1")
    if stop_after == 4:
        return finish(nc, S, out_d, dbg_out, None)
    S.barrier()
    A.free("hT")

    o_d = A.alloc("o_d", [128, 16, 512], BF16)
    r_od = [Res("od%d" % t) for t in range(16)]
    matsD, halfbD, tokbD, st5D = [], [], [], []
    for d in range(2):
        mats = {}
        for nm in ["C", "B", "intra"] + ["m%d" % i for i in range(8)]:
            mats[nm] = (A.alloc("m%d_%s" % (d, nm), [128, 8, 128], BF16), Res("m%d_%s" % (d, nm)))
        halfb = {}
        for nm in ["E", "kkS", "nbE"]:
            halfb[nm] = (A.alloc("h%d_%s" % (d, nm), [128, 4, 128], F32), Res("h%d_%s" % (d, nm)))
        tokb = {}
        for nm in ["qd", "kbg", "vb"]:
            tokb[nm] = (A.alloc("t%d_%s" % (d, nm), [128, 512], BF16), Res("t%d_%s" % (d, nm)))
        matsD.append(mats)
        halfbD.append(halfb)
        tokbD.append(tokb)
        st5D.append((A.alloc("st5_%d" % d, [128, 64], F32), Res("st5_%d" % d)))
    per = []
    for d in range(2):
        pd = {}
        for nm, shp, dt in [("WT", [128, 8, 128], BF16), ("U", [128, 512], F32), ("QdT0", [128, 8, 128], BF16),
                            ("QdT1", [128, 8, 128], BF16), ("intraT", [128, 8, 128], BF16), ("ktl0", [128, 512], BF16),
                            ("ktl1", [128, 512], BF16), ("egl0", [128, 16], F32), ("egl1", [128, 16], F32),
                            ("S", [128, 512], F32), ("Sbf", [128, 512], BF16), ("ubf", [128, 512], BF16),
                            ("tmpS", [128, 512], F32)]:
            pd[nm] = (A.alloc("d%d_%s" % (d, nm), shp, dt), Res("d%d_%s" % (d, nm)))
        per.append(pd)
        memset("pool", pd["S"][0], 0.0, w=[pd["S"][1]])
        memset("pool", pd["Sbf"][0], 0.0, w=[pd["Sbf"][1]])
        memset("pool", pd["ubf"][0], 0.0, w=[pd["ubf"][1]])
    mres = [[[Res("mh%d_%d_%d" % (d, i, hf)) for hf in range(2)] for i in range(8)] for d in range(2)]
    evc = [0]

    def evac(out, in_, r, w):
        cp("dve" if evc[0] % 4 == 3 else "act", out, in_, r=r, w=w)
        evc[0] += 1

    od_seen = set()
    CUT = float(os.environ.get("DN_CUT", 99))

    def dn_visit(t, d, par):
        sfx = "f" if d == 0 else "b"
        lat = t >= 2
        pd = per[d]
        mats, halfb, tokb = matsD[d], halfbD[d], tokbD[d]
        st5, r_st5 = st5D[d]
        tsl = slice(t * 128, (t + 1) * 128)
        gd_ = gall[:, t, d * 8:(d + 1) * 8]
        bd_ = ball[:, t, d * 8:(d + 1) * 8]
        Gc, eG, bEG, ktw, nbeta = [st5[:, i * 8:(i + 1) * 8] for i in range(5)]
        egl, r_egl = pd["egl%d" % par]
        psg, _, r_psg = nb()
        for (dst, lh) in [(psg[:, 0:8], cB("tri_" + sfx)), (psg[:, 8:16], cB("blk")),
                          (psg[0:64, 16:24], cB("blk")[:, 0:64]), (psg[0:64, 24:32], cB("blk")[:, 64:128])]:
            for pc in range(3):
                mm(dst, lh, gp[pc][:, t, d * 8:(d + 1) * 8], start=(pc == 0), stop=(pc == 2),
                   r=[r_cb, r_gp], w=[r_psg])
        cp("dve", Gc, psg[:, 0:8], r=[r_psg], w=[r_st5])
        act(eG, Gc, AF.Exp, r=[r_st5], w=[r_st5])
        tt("dve", bEG, bd_, eG, ALU.mult, r=[r_g[t], r_st5], w=[r_st5])
        tt("dve", ktw, psg[:, 8:16], Gc, ALU.subtract, r=[r_psg, r_st5], w=[r_st5])
        act(ktw, ktw, AF.Exp, r=[r_st5], w=[r_st5])
        ts("dve", nbeta, bd_, -1.0, ALU.mult, r=[r_g[t]], w=[r_st5])
        act(egl[0:64, 0:16], psg[0:64, 16:32], AF.Exp, r=[r_psg], w=[r_egl])
        if CUT <= 1:
            return
        yield
        _, pqk, r_pqk = nb()
        _, pv, r_pv = nb()
        for c in range(4 if lat else 0):
            tr(pqk[:, c * 128:(c + 1) * 128], pre[:, c, tsl], cB("ident"), r=[r_pre[c], r_cb], w=[r_pqk])
        for c in range(4, 8):
            tr(pqk[:, c * 128:(c + 1) * 128], pre[:, c, tsl], cB("ident"), r=[r_pre[c], r_cb], w=[r_pqk])
        for c in range(8, 12):
            tr(pv[:, (c - 8) * 128:(c - 7) * 128], pre[:, c, tsl], cB("ident"), r=[r_pre[c], r_cb], w=[r_pv])
        qd, r_qd = tokb["qd"]
        kbg, r_kbg = tokb["kbg"]
        vb, r_vb = tokb["vb"]
        ktl, r_ktl = pd["ktl%d" % par]
        h3 = lambda ap: ap.rearrange("p (h d) -> p h d", h=8)
        if lat:
            tt("dve", h3(qd), h3(pqk[:, 0:512]), bc(eG.unsqueeze(2), [128, 8, 64]), ALU.mult, r=[r_pqk, r_st5], w=[r_qd])
        tt("dve", h3(kbg), h3(pqk[:, 512:1024]), bc(bEG.unsqueeze(2), [128, 8, 64]), ALU.mult, r=[r_pqk, r_st5], w=[r_kbg])
        tt("dve", h3(ktl), h3(pqk[:, 512:1024]), bc(ktw.unsqueeze(2), [128, 8, 64]), ALU.mult, r=[r_pqk, r_st5], w=[r_ktl])
        tt("dve", h3(vb), h3(pv[:, 0:512]), bc(bd_.unsqueeze(2), [128, 8, 64]), ALU.mult, r=[r_pv, r_g[t]], w=[r_vb])
        QdT, r_QdT = pd["QdT%d" % par]
        if CUT <= 2:
            return
        yield
        C_, r_C = mats["C"]
        intra, r_intra = mats["intra"]
        E_, r_E = halfb["E"]
        kkS, r_kkS = halfb["kkS"]
        nbE, r_nbE = halfb["nbE"]
        for half in range(2):
            hs = slice(half, 8, 2)
            pg, _, r_pg = nb()
            for hh in range(4):
                h = half + 2 * hh
                for pc in range(3):
                    mm(pg[:, hh * 128:(hh + 1) * 128], bc(gp[pc][:, t, d * 8 + h:d * 8 + h + 1], [128, 128]),
                       cB("tri_" + sfx), start=(pc == 0), stop=False, r=[r_cb, r_gp], w=[r_pg])
                mm(pg[:, hh * 128:(hh + 1) * 128], cB("ident"), cB("big_" + sfx), start=False, stop=True,
                   r=[r_cb], w=[r_pg])
            yield
            for hh in range(4):
                h = half + 2 * hh
                act(E_[:, hh, :], pg[:, hh * 128:(hh + 1) * 128], AF.Exp, scale=-1.0, bias=Gc[:, h:h + 1],
                    r=[r_pg, r_st5], w=[r_E])
            yield
            pk, _, r_pk = nb()
            for hh in range(4):
                h = half + 2 * hh
                KTh = pre[(h % 2) * 64:(h % 2) * 64 + 64, 4 + h // 2, tsl]
                mm(pk[:, hh * 128:(hh + 1) * 128], KTh, KTh, r=[r_pre[4 + h // 2]], w=[r_pk])
            yield
            tt("dve", kkS, pk[:, 0:512].rearrange("p (h j) -> p h j", h=4),
               bc(cB("strict_" + sfx).unsqueeze(1), [128, 4, 128]), ALU.mult, r=[r_pk, r_cb], w=[r_kkS])
            tt("pool", nbE, E_, bc(nbeta[:, hs].unsqueeze(2), [128, 4, 128]), ALU.mult, r=[r_E, r_st5], w=[r_nbE])
            tt("pool", C_[:, hs, :], nbE, kkS, ALU.mult, r=[r_nbE, r_kkS], w=[r_C])
            if lat:
                pq, _, r_pq = nb()
                for hh in range(4):
                    h = half + 2 * hh
                    KTh = pre[(h % 2) * 64:(h % 2) * 64 + 64, 4 + h // 2, tsl]
                    QTh = pre[(h % 2) * 64:(h % 2) * 64 + 64, h // 2, tsl]
                    mm(pq[:, hh * 128:(hh + 1) * 128], QTh, KTh, r=[r_pre[4 + h // 2], r_pre[h // 2]], w=[r_pq])
                tt("dve", intra[:, hs, :], E_, pq[:, 0:512].rearrange("p (h j) -> p h j", h=4), ALU.mult,
                   r=[r_E, r_pq], w=[r_intra])
        yield
        if lat:
            _, pq2, r_pq2 = nb()
            for h in range(8):
                tr(pq2[0:64, h * 128:(h + 1) * 128], qd[:, h * 64:(h + 1) * 64], cB("ident"), r=[r_qd, r_cb], w=[r_pq2])
            evac(QdT[0:64], pq2[0:64, :].rearrange("p (h t) -> p h t", h=8), r=[r_pq2], w=[r_QdT])
        for half_ in range(2):
            hs_ = slice(4 * half_, 4 * half_ + 4)
            tt("pool", mats["m0"][0][:, hs_, :], C_[:, hs_, :], bc(cB("m8").unsqueeze(1), [128, 4, 128]), ALU.mult,
               r=[r_C, r_cb], w=[mres[d][0][half_]])
        yield "endA"
        B_, r_B = mats["B"]
        _, pt1, r_pt1 = nb()
        for h in range(8):
            tr(pt1[:, h * 128:(h + 1) * 128], C_[:, h, :], cB("ident"), r=[r_C, r_cb], w=[r_pt1])
        evac(B_, pt1.rearrange("p (h j) -> p h j", h=8), r=[r_pt1], w=[r_B])
        intraT, r_intraT = pd["intraT"]
        if lat:
            _, pt2, r_pt2 = nb()
            for h in range(8):
                tr(pt2[:, h * 128:(h + 1) * 128], intra[:, h, :], cB("ident"), r=[r_intra, r_cb], w=[r_pt2])
            evac(intraT, pt2.rearrange("p (h j) -> p h j", h=8), r=[r_pt2], w=[r_intraT])
        yield
        M = [(mats["m%d" % i][0], mres[d][i]) for i in range(8)]

        def masked(dst, src, mname, eng="pool"):
            (d_, r_d), (s_, r_s) = dst, src
            for half in range(2):
                hs_ = slice(4 * half, 4 * half + 4)
                tt(eng, d_[:, hs_, :], s_[:, hs_, :], bc(cB(mname).unsqueeze(1), [128, 4, 128]), ALU.mult,
                   r=[r_s, r_cb], w=[r_d[half]])

        def plus_ident(dst, src):
            (d_, r_d), (s_, r_s) = dst, src
            for half in range(2):
                hs_ = slice(4 * half, 4 * half + 4)
                tt("dve", d_[:, hs_, :], s_[:, hs_, :], bc(cB("ident").unsqueeze(1), [128, 4, 128]), ALU.add,
                   r=[r_s[half], r_cb], w=[r_d[half]])

        IDENT = (None, None)

        def stage(dst, terms):
            d_, r_d = dst
            mterms = [t_ for t_ in terms if t_[0][0] is not None]
            aterms = [t_ for t_ in terms if t_[0][0] is None]
            for half in range(2):
                pb_, _, r_pb_ = nb()
                for hh in range(4):
                    h = 4 * half + hh
                    for ti, ((l_, r_l), (x_, r_x)) in enumerate(mterms):
                        mm(pb_[:, hh * 128:(hh + 1) * 128], l_[:, h, :], x_[:, h, :], start=(ti == 0), stop=(ti == len(mterms) - 1),
                           r=[r_l[half], r_x[half]], w=[r_pb_])
                dsth = d_[:, 4 * half:4 * half + 4, :]
                psv = pb_[:, 0:512].rearrange("p (h j) -> p h j", h=4)
                if aterms:
                    (_, _), (xa_, r_xa) = aterms[0]
                    tt("dve", dsth, psv, xa_[:, 4 * half:4 * half + 4, :], ALU.add, r=[r_pb_, r_xa[half]], w=[r_d[half]])
                else:
                    cp("act", dsth, psv, r=[r_pb_], w=[r_d[half]])

        Cm, Bm = (C_, r_C), (B_, r_B)
        masked(M[1], Bm, "m8", eng="dve")
        yield
        stage(M[2], [(M[1], M[0])])
        yield
        stage(M[3], [(M[0], M[1])])
        yield
        plus_ident(M[4], M[0])
        yield
        plus_ident(M[5], M[1])
        yield
        stage(M[6], [(M[3], M[2])])
        yield
        stage(M[7], [(M[2], M[3])])
        yield
        stage(M[0], [(M[3], M[4]), (IDENT, M[4])])
        yield
        stage(M[1], [(M[2], M[5]), (IDENT, M[5])])
        yield
        stage(M[4], [(M[7], M[0]), (IDENT, M[0])])
        yield
        stage(M[5], [(M[6], M[1]), (IDENT, M[1])])
        yield
        X, XT = M[4], M[5]
        freeb = [M[0], M[1]]
        for lvl, mname in enumerate(["l16", "l32", "l64"]):
            masked(M[2], Cm, mname, eng="dve")
            yield
            stage(M[6], [(M[2], XT)])
            yield
            if lvl < 2:
                masked(M[3], Bm, mname, eng="dve")
                yield
                stage(M[7], [(M[3], X)])
                yield
            XTn, Xn = freeb
            stage(XTn, [(X, M[6]), (IDENT, XT)])
            yield
            if lvl < 2:
                stage(Xn, [(XT, M[7]), (IDENT, X)])
                yield
            freeb = [XT, X]
            X, XT = Xn, XTn
        TinvT, r_Ti = XT
        yield
        WT, r_WT = pd["WT"]
        U, r_U = pd["U"]
        for half in range(2):
            hs = slice(4 * half, 4 * half + 4)
            pw, _, r_pw = nb()
            for hh in range(4):
                h = 4 * half + hh
                mm(pw[0:64, hh * 128:(hh + 1) * 128], kbg[:, h * 64:(h + 1) * 64], TinvT[:, h, :], r=[r_kbg, r_Ti[half]], w=[r_pw])
            evac(WT[0:64, hs, :], pw[0:64, 0:512].rearrange("p (h j) -> p h j", h=4), r=[r_pw], w=[r_WT])
        pu, _, r_pu = nb()
        for h in range(8):
            mm(pu[:, h * 64:(h + 1) * 64], TinvT[:, h, :], vb[:, h * 64:(h + 1) * 64], r=[r_Ti[h // 4], r_vb], w=[r_pu])
        evac(U, pu[:, 0:512], r=[r_pu], w=[r_U])
        yield "endB"
        S_, r_S = pd["S"]
        Sbf, r_Sbf = pd["Sbf"]
        ubf, r_ubf = pd["ubf"]
        tmpS, r_tmpS = pd["tmpS"]
        for ch in ([0, 1] if d == 0 else [1, 0]):
            rows = slice(ch * 64, ch * 64 + 64)
            p1, _, r_p1 = nb()
            for h in range(8):
                mm(p1[:, h * 64:(h + 1) * 64], WT[0:64, h, :], Sbf[0:64, h * 64:(h + 1) * 64], r=[r_WT, r_Sbf], w=[r_p1])
            tt("dve", ubf[rows, :], U[rows, :], p1[rows, 0:512], ALU.subtract, r=[r_U, r_p1], w=[r_ubf])
            yield
            if lat:
                l = t - 2
                p2, _, r_p2 = nb()
                for h in range(8):
                    mm(p2[:, h * 64:(h + 1) * 64], QdT[0:64, h, :], Sbf[0:64, h * 64:(h + 1) * 64], start=True, stop=False,
                       r=[r_QdT, r_Sbf], w=[r_p2])
                    mm(p2[:, h * 64:(h + 1) * 64], intraT[:, h, :], ubf[:, h * 64:(h + 1) * 64], start=False, stop=True,
                       r=[r_intraT, r_ubf], w=[r_p2])
                if (l, ch) not in od_seen:
                    od_seen.add((l, ch))
                    cp("act", o_d[rows, l, :], p2[rows, 0:512], r=[r_p2], w=[r_od[l]])
                else:
                    tt("dve", o_d[rows, l, :], o_d[rows, l, :], p2[rows, 0:512], ALU.add, r=[r_p2, r_od[l]], w=[r_od[l]])
            p3, _, r_p3 = nb()
            for h in range(8):
                mm(p3[0:64, h * 64:(h + 1) * 64], ktl[rows, h * 64:(h + 1) * 64], ubf[rows, h * 64:(h + 1) * 64],
                   r=[r_ktl, r_ubf], w=[r_p3])
            tt("pool", h3(tmpS[0:64, :]), h3(S_[0:64, :]), bc(egl[0:64, ch * 8:(ch + 1) * 8].unsqueeze(2), [64, 8, 64]),
               ALU.mult, r=[r_S, r_egl], w=[r_tmpS])
            tt("dve", Sbf[0:64, :], tmpS[0:64, :], p3[0:64, 0:512], ALU.add, r=[r_tmpS, r_p3], w=[r_Sbf])
            tt("dve", S_[0:64, :], tmpS[0:64, :], p3[0:64, 0:512], ALU.add, r=[r_tmpS, r_p3], w=[r_S])
            yield

    fwd_seq = list(range(NT))
    bwd_seq = [1, 0] + list(range(NT - 1, 1, -1))
    nsteps = int(os.environ.get("DN_STEPS", NT))
    def advance(pairs):
        live = list(pairs)
        while live:
            for item in list(live):
                g_, stop = item
                try:
                    v = next(g_)
                except StopIteration:
                    live.remove(item)
                    continue
                if stop is not None and v == stop:
                    live.remove(item)

    visits = [(dn_visit(fwd_seq[s_], 0, s_ % 2), dn_visit(bwd_seq[s_], 1, s_ % 2)) for s_ in range(nsteps)]
    advance([(g_, "endA") for g_ in visits[0]])
    for s_ in range(nsteps):
        advance([(g_, "endB") for g_ in visits[s_]])
        nxt = [(g_, "endA") for g_ in visits[s_ + 1]] if s_ + 1 < nsteps else []
        advance(nxt + [(g_, None) for g_ in visits[s_]])
        if s_ == 1 and "Sctx" in debug:
            dump("Sf", per[0]["S"][0][0:64], per[0]["S"][1])
            dump("Sb", per[1]["S"][0][0:64], per[1]["S"][1])
    if "o_d" in debug:
        r_all = Res("dall4")
        S.wait_only("pool", [r.w for r in r_od])
        dump("o_d", o_d, r_all)
    for d in range(2):
        for nm in per[d]:
            A.free("d%d_%s" % (d, nm))
    for d in range(2):
        for nm in matsD[d]:
            A.free("m%d_%s" % (d, nm))
        for nm in halfbD[d]:
            A.free("h%d_%s" % (d, nm))
        for nm in tokbD[d]:
            A.free("t%d_%s" % (d, nm))
        A.free("st5_%d" % d)
    A.free("pre")
    if stop_after == 5:
        return finish(nc, S, out_d, dbg_out, None)
    S.barrier()
    A.free("gall", "ball", "gp0", "gp1", "gp2")

    NW = 4
    wbra = A.alloc("wbra", [128, 4, D], BF16)
    wbrd = A.alloc("wbrd", [128, 4, D], BF16)
    r_wbra, r_wbrd = Res("wbra"), Res("wbrd")
    wpool = [A.alloc("wp%d" % i, [128, 4096], BF16) for i in range(NW)]
    r_wpool = [Res("wp%d" % i) for i in range(NW)]
    wpi = [0]

    def wslab(src, a):
        i = wpi[0] % NW
        wpi[0] += 1
        v = wpool[i].rearrange("p (a b) -> p a b", a=a)
        S.dma("pool", [(v, src)], w=[r_wpool[i]])
        return v, r_wpool[i]

    w1_v = w1_d.rearrange("(k p) n -> p k n", p=128)
    w2_v = w2_d.rearrange("(f p) n -> p f n", p=128)
    wout_v = wout_d.rearrange("(k p) n -> p k n", p=128)
    wbra_v = wbra_d.rearrange("(k p) n -> p k n", p=128)
    wbrd_v = wbrd_d.rearrange("(k p) n -> p k n", p=128)

    onec = salloc(1)
    memset("pool", onec, 1.0, w=[r_small])
    xall = A.alloc("xall", [128, 16, D], F32)
    r_xall = [Res("xall%d" % i) for i in range(16)]
    hTb = A.alloc("hTb", [128, 8, 512], BF16)
    r_hTb = [Res("hTb%d" % i) for i in range(4)]
    ydT = A.alloc("ydT", [128, 4, 512], BF16)
    r_ydT = [Res("ydT%d" % i) for i in range(4)]
    yTb = A.alloc("yTb", [128, 8, 512], BF16)
    r_yTb = [Res("yTb%d" % f) for f in range(8)]
    NS6 = 4
    es6 = [(A.alloc("f_e%d" % i, [128, 512], F32), Res("f_e%d" % i)) for i in range(NS6)]
    ts6 = [(A.alloc("f_t%d" % i, [128, 512], F32), Res("f_t%d" % i)) for i in range(NS6)]
    ydts = [(A.alloc("ydt%d" % i, [128, 512], BF16), Res("ydt%d" % i)) for i in range(2)]
    junk6 = A.alloc("junk6", [128, D], BF16)
    r_junk6 = Res("junk6")
    xn6 = [A.alloc("xn6_%d" % i, [128, D], BF16) for i in range(2)]
    r_xn6 = [Res("xn6_%d" % i) for i in range(2)]
    st6 = A.alloc("st6", [128, 32], F32)
    r_st6 = Res("st6")

    def sigmoid_act(dst, r_dst, src_ps, r_src):
        act(dst, src_ps, AF.Exp, scale=-1.0, r=[r_src], w=[r_dst])
        act(dst, dst, AF.Ln, bias=onec, r=[r_dst, r_small], w=[r_dst])
        act(dst, dst, AF.Exp, scale=-1.0, r=[r_dst], w=[r_dst])

    nblk = int(os.environ.get("P6_BLOCKS", 4))
    for b in range(nblk):
        items = []
        for i in range(4):
            lt = 4 * b + i
            items.append(dict(src=xall[:, lt, :], r_src=r_xall[lt], dstT=hTb, tok=slice(i * 128, (i + 1) * 128), r_dst=r_hTb[i],
                              A=coef[:, 0, :], B=(lambda c: modv[:, c, 0:1]), bufs=(junk6, r_junk6, xn6[i % 2], r_xn6[i % 2]), i=lt,
                              pre=(lambda lt=lt: S.dma("sp", [(xall[:, lt, :], x_d[lt * 128:(lt + 1) * 128, :])], w=[r_xall[lt]]))))
        norm_transpose_seq(items)
        wz, r_wz = wslab(win_v[:, :, 2304:2816], 8)
        pszs = {}

        def z_proj(i):
            psz, _, r_psz = nb()
            for k in range(8):
                mm(psz[:, 0:512], hTb[:, k, i * 128:(i + 1) * 128], wz[:, k, :], start=(k == 0), stop=(k == 7),
                   r=[r_hTb[i], r_wz], w=[r_psz])
            pszs[i] = (psz, r_psz)

        z_proj(0)
        for i in range(4):
            lt = 4 * b + i
            if i + 1 < 4:
                z_proj(i + 1)
            sa, sb_ = 2 * (i % 2), 2 * (i % 2) + 1
            e1, r_e1 = es6[sa]
            t1, r_t1 = ts6[sa]
            sq6, r_sq6 = es6[sb_]
            t2, r_t2 = ts6[sb_]
            ydt, r_ydt = ydts[i % 2]
            psz, r_psz = pszs[i]
            sigmoid_act(e1, r_e1, psz[:, 0:512], r_psz)
            tt("dve", e1, psz[:, 0:512], e1, ALU.mult, r=[r_psz, r_e1], w=[r_e1])
            od3 = o_d[:, lt, :].rearrange("p (h d) -> p h d", h=8)
            act(sq6, o_d[:, lt, :], AF.Square, r=[r_od[lt]], w=[r_sq6])
            ssq = st6[:, 0:8]
            S.op("dve", lambda e, ssq=ssq, sq6=sq6: e.tensor_reduce(ssq, sq6.rearrange("p (h d) -> p h d", h=8), AX.X, ALU.add),
                 r=[r_sq6], w=[r_st6])
            rstd_from_ss(st6[:, 16:24], ssq, 1.0 / 64, st6[:, 8:16], r_st6, r_st6, r_st6)
            tt("dve", t1.rearrange("p (h d) -> p h d", h=8), od3, bc(st6[:, 16:24].unsqueeze(2), [128, 8, 64]), ALU.mult,
               r=[r_od[lt], r_st6], w=[r_t1])
            tt("dve", t2.rearrange("p (h d) -> p h d", h=8), t1.rearrange("p (h d) -> p h d", h=8),
               bc(dng_b.unsqueeze(1), [128, 8, 64]), ALU.mult, r=[r_t1, r_small], w=[r_t2])
            tt("dve", ydt, t2, e1, ALU.mult, r=[r_t2, r_e1], w=[r_ydt])
            _, pbb, r_pb = nb()
            for c in range(4):
                tr(pbb[:, c * 128:(c + 1) * 128], ydt[:, c * 128:(c + 1) * 128], cB("ident"), r=[r_ydt, r_cb], w=[r_pb])
            cp("act", ydT[:, :, i * 128:(i + 1) * 128], pbb[:, 0:512].rearrange("p (c t) -> p c t", c=4),
               r=[r_pb], w=[r_ydT[i]])
        bsl = slice(b * 512, (b + 1) * 512)
        for f in range(8):
            if f % 4 == 0:
                wga, r_wga = wslab(win_v[:, :, 2848 + f * 128:2848 + f * 128 + 512], 8)
                wgd, r_wgd = wslab(win_v[:, :, 3872 + f * 128:3872 + f * 128 + 512], 8)
            if f == 0 and b == 0:
                S.dma("pool", [(wbra, wbra_v)], w=[r_wbra])
                S.dma("pool", [(wbrd, wbrd_v)], w=[r_wbrd])
            fc = (f % 4) * 128
            sa, sb_ = (2 * f) % NS6, (2 * f + 1) % NS6
            (e1, r_e1), (t1, r_t1) = es6[sa], ts6[sa]
            (e2, r_e2), (t2, r_t2) = es6[sb_], ts6[sb_]
            for (wg, r_wg, wb_, r_wb_, src, r_src, eb, r_eb, tb_, r_tb) in [
                    (wga, r_wga, wbra, r_wbra, yattnT[:, :, bsl], r_yat[4 * b:4 * b + 4], e1, r_e1, t1, r_t1),
                    (wgd, r_wgd, wbrd, r_wbrd, ydT, r_ydT, e2, r_e2, t2, r_t2)]:
                psg_, _, r_psg_ = nb()
                for k in range(8):
                    mm(psg_[:, 0:512], wg[:, k, fc:fc + 128], hTb[:, k, :], start=(k == 0), stop=(k == 7),
                       r=[r_wg] + r_hTb, w=[r_psg_])
                act(eb, psg_[:, 0:512], AF.Sigmoid, r=[r_psg_], w=[r_eb])
                psb_, _, r_psb_ = nb()
                for k in range(4):
                    mm(psb_[:, 0:512], wb_[:, k, f * 128:(f + 1) * 128], src[:, k, :], start=(k == 0), stop=(k == 3),
                       r=[r_wb_] + list(r_src), w=[r_psb_])
                tt("dve", tb_, psb_[:, 0:512], eb, ALU.mult, r=[r_psb_, r_eb], w=[r_tb])
            tt("dve", yTb[:, f, :], t1, t2, ALU.add, r=[r_t1, r_t2], w=[r_yTb[f]])
        for half in range(2):
            wo, r_wo = wslab(wout_v[:, :, half * 512:(half + 1) * 512], 8)
            for i in range(4):
                lt = 4 * b + i
                pso_, _, r_pso_ = nb()
                for f in range(8):
                    mm(pso_[:, 0:512], yTb[:, f, i * 128:(i + 1) * 128], wo[:, f, :], start=(f == 0), stop=(f == 7),
                       r=[r_yTb[f], r_wo], w=[r_pso_])
                t1, r_t1 = ts6[(half * 4 + i) % NS6]
                tt("dve", t1, pso_[:, 0:512], gtrow[:, 0, half * 512:(half + 1) * 512], ALU.mult,
                   r=[r_pso_, r_gtrow], w=[r_t1])
                xs = xall[:, lt, half * 512:(half + 1) * 512]
                tt("dve", xs, xs, t1, ALU.add, r=[r_t1, r_xall[lt]], w=[r_xall[lt]])
    if "xnew_all" in debug:
        r_all = Res("dall7")
        S.wait_only("pool", [r.w for r in r_xall])
        dump("xnew_all", xall, r_all)
    S.barrier()
    A.free("wbra", "wbrd", "hTb", "ydT", "yTb", "ydt0", "ydt1", "yattnT", "o_d", "f_t0", "f_t1", "f_t2", "f_t3", "f_e2", "f_e3")

    h2Th = [A.alloc("h2T%d" % i, [128, 8, SEQ // 2], BF16) for i in range(2)]
    r_h2T = [Res("h2T%d" % i) for i in range(16)]
    m1b = [[A.alloc("m1b%d_%d" % (i, tb), [128, 4, 512], BF16) for tb in range(4)] for i in range(2)]
    r_m1b = [[Res("m1b%d_%d" % (i, tb)) for tb in range(4)] for i in range(2)]
    items = []
    for lt in range(16):
        items.append(dict(src=xall[:, lt, :], r_src=r_xall[lt], dstT=h2Th[lt // 8], tok=slice((lt % 8) * 128, (lt % 8 + 1) * 128),
                          r_dst=r_h2T[lt], A=coef[:, 2, :], B=(lambda c: modv[:, 24 + c, 0:1]),
                          bufs=(junk6, r_junk6, xn6[lt % 2], r_xn6[lt % 2]), i=lt))
    norm_transpose_seq(items)
    e1, r_e1 = es6[0]
    e2, r_e2 = es6[1]
    out_toks = []
    slabs = {}

    def mlp1(j):
        w1s, r_w1s = wslab(w1_v[:, :, j * 512:(j + 1) * 512], 8)
        w2s, r_w2s = wslab(w2_v[:, 4 * j:4 * j + 4, :], 4)
        tt("dve", w2s, w2s, bc(gtrow[:, 1, :].unsqueeze(1), [128, 4, D]), ALU.mult, r=[r_w2s, r_gtrow], w=[r_w2s])
        slabs[j] = (w2s, r_w2s)
        m1j, r_m1j = m1b[j % 2], r_m1b[j % 2]
        for tb in range(4):
            for cc in range(4):
                ps1, _, r_ps1 = nb()
                for k in range(8):
                    mm(ps1[:, 0:512], w1s[:, k, cc * 128:(cc + 1) * 128], h2Th[tb // 2][:, k, (tb % 2) * 512:(tb % 2 + 1) * 512],
                       start=(k == 0), stop=(k == 7), r=[r_w1s] + r_h2T[4 * tb:4 * tb + 4], w=[r_ps1])
                sqb_, r_sqb_ = (e1, r_e1) if cc % 2 == 0 else (e2, r_e2)
                act(sqb_, ps1[:, 0:512], AF.Square, r=[r_ps1], w=[r_sqb_])
                stt("dve", m1j[tb][:, cc, :], ps1[:, 0:512], 0.0, sqb_, ALU.is_gt, ALU.mult,
                    r=[r_ps1, r_sqb_], w=[r_m1j[tb]])

    def mlp2(j):
        w2s, r_w2s = slabs[j]
        m1j, r_m1j = m1b[j % 2], r_m1b[j % 2]
        for lt in range(16):
            for half in range(2):
                ps2, _, r_ps2 = nb()
                for cc in range(4):
                    mm(ps2[:, 0:512], m1j[lt // 4][:, cc, (lt % 4) * 128:(lt % 4 + 1) * 128], w2s[:, cc, half * 512:(half + 1) * 512],
                       start=(cc == 0), stop=(cc == 3), r=[r_m1j[lt // 4], r_w2s], w=[r_ps2])
                xs = xall[:, lt, half * 512:(half + 1) * 512]
                tt("dve", xs, xs, ps2[:, 0:512], ALU.add, r=[r_ps2, r_xall[lt]], w=[r_xall[lt]])
            if j == 7:
                out_toks.append(S.dma("sp", [(out_d[lt * 128:(lt + 1) * 128, :], xall[:, lt, :])], r=[r_xall[lt]]))

    mlp1(0)
    for j in range(8):
        if j + 1 < 8:
            mlp1(j + 1)
        mlp2(j)
    return finish(nc, S, out_d, dbg_out, out_toks)


def finish(nc, S, out_d, dbg_out, out_toks):
    if out_toks:
        S.wait_only("sp", out_toks)
    S.barrier()
    n = S.emit()
    DBG["ninst"] = n
    DBG["nsem"] = S.nsem
    DBG["dbg_out"] = dbg_out
    return nc


def make_in_maps(inputs):
    f = lambda a: np.ascontiguousarray(np.asarray(a, dtype=np.float32))
    x = f(inputs["x"]); c = f(inputs["c"]); ctx = f(inputs["ctx"]); c_ctx = f(inputs["c_ctx"])
    B = x.shape[0]
    shared = {
        "badaT": f(inputs["b_ada"][0].reshape(48, 128).T),
        "bada": f(inputs["b_ada"][0]),
        "g1T": f(inputs["g_norm1"][0].reshape(8, 128).T),
        "g2T": f(inputs["g_norm2"][0].reshape(8, 128).T),
        "w_ada": f(inputs["w_ada"][0]),
        "w_in": f(inputs["w_in"][0]),
        "q_norm_g": f(inputs["q_norm_g"][0]),
        "k_norm_g": f(inputs["k_norm_g"][0]),
        "attn_sink": f(inputs["attn_sink"][0]),
        "convT": f(np.asarray(inputs["conv_w"][0]).reshape(3, 12, 128).transpose(2, 1, 0).reshape(128, 36)),
        "a_log": f(np.asarray(inputs["a_log"][0]).reshape(16)),
        "dt_bias": f(np.asarray(inputs["dt_bias"][0]).reshape(16)),
        "dn_norm_g": f(inputs["dn_norm_g"][0]),
        "w_br_attn": f(inputs["w_br_attn"][0]),
        "w_br_dn": f(inputs["w_br_dn"][0]),
        "w_out": f(inputs["w_out"][0]),
        "w_mlp1": f(inputs["w_mlp1"][0]),
        "w_mlp2": f(inputs["w_mlp2"][0]),
        "consts": CONST_ARR,
        "rope_cos": ROPE_COS,
        "rope_sin": ROPE_SIN,
    }
    maps = []
    for b in range(B):
        m = dict(shared)
        m["x"] = x[b]
        m["ctx"] = ctx[b]
        cc = np.stack([c[b].reshape(8, 128), c_ctx.reshape(8, 128)], axis=-1)
        m["cT"] = f(cc.transpose(1, 0, 2).reshape(128, 16))
        maps.append(m)
    return maps


def kernel(**inputs):
    maps = make_in_maps(inputs)
    nc = build()
    res = run_bass_kernel_spmd(nc, maps, core_ids=list(range(8)))
    out = np.stack([np.asarray(r["out"]) for r in res.results], axis=0)
    return out.astype(np.float32)
```

```python
import os
import numpy as np
import concourse.bass as bass
import concourse.mybir as mybir
from concourse.bass_utils import run_bass_kernel_spmd

F32 = mybir.dt.float32
BF16 = mybir.dt.bfloat16
U8 = mybir.dt.uint8
ALU = mybir.AluOpType
AF = mybir.ActivationFunctionType
AX = mybir.AxisListType

D = 1024
SEQ = 2048
CTX = 256
NT = 18
TOK = NT * 128
IN_W = 4896
EPS = 1e-6
BIG = 30000.0
ATTACH_WAIT = True


class Res:
    __slots__ = ("name", "w", "r", "dsem", "dcount", "excl")

    def __init__(self, name, excl=False):
        self.name = name
        self.excl = excl
        self.w = None
        self.r = {}
        self.dsem = None
        self.dcount = 0


class Sched:
    ENGS = ("pe", "act", "dve", "pool", "sp")

    def __init__(self, nc):
        self.nc = nc
        self.eng = {"pe": nc.tensor, "act": nc.scalar, "dve": nc.vector,
                    "pool": nc.gpsimd, "sp": nc.sync}
        self.streams = {e: [] for e in self.ENGS}
        self.seen = {e: {} for e in self.ENGS}
        self.last = {e: None for e in self.ENGS}
        self.dtoks = []
        self.esem = {}
        self.nsem = 0
        self.snap = {e: [] for e in self.ENGS}
        self.dsnap = {}

    def _sem(self, name):
        self.nsem += 1
        return self.nc.alloc_semaphore(name=name)

    def _collect(self, eng, r, w):
        deps = []
        for res in r:
            if res.w is not None:
                deps.append((res.w, True))
            if res.excl:
                for k_, t in res.r.items():
                    if k_ != eng:
                        deps.append((t, False))
        for res in w:
            if res.w is not None:
                deps.append((res.w, False))
            for t in res.r.values():
                deps.append((t, False))
        best = {}
        for t, raw in deps:
            if t[0] == "e":
                if t[1] == eng and (eng == "pe" or not raw):
                    continue
                key = ("e", t[1])
            else:
                key = ("d", t[1])
            val = t[2]
            if best.get(key, -1) < val:
                best[key] = val
        waits = []
        seen = self.seen[eng]
        for key, val in sorted(best.items(), key=lambda kv: -kv[1]):
            if seen.get(key, -1) >= val:
                continue
            seen[key] = val
            waits.append((key, val))
            hist = self.snap[key[1]][val] if key[0] == "e" else self.dsnap.get((key[1], val))
            if hist:
                for k2, v2 in hist.items():
                    if k2 != ("e", eng) and seen.get(k2, -1) < v2:
                        seen[k2] = v2
        return waits

    def op(self, eng, fn, r=(), w=()):
        r = [x for x in r if x is not None]
        w = [x for x in w if x is not None]
        waits = self._collect(eng, r, w)
        st = self.streams[eng]
        idx = len(st)
        tok = ("e", eng, idx)
        st.append({"fn": fn, "waits": waits, "inc": False, "dma": None})
        self.snap[eng].append(dict(self.seen[eng]))
        self.last[eng] = tok
        for res in r:
            res.r[eng] = tok
        for res in w:
            res.w = tok
            res.r = {}
        return tok

    def dma(self, eng, pairs, r=(), w=()):
        r = [x for x in r if x is not None]
        w = [x for x in w if x is not None]
        waits = self._collect(eng, r, w)
        key_res = w[0] if w else r[0]
        if key_res.dsem is None:
            key_res.dsem = self._sem("d_" + key_res.name)
        key_res.dcount += 16 * len(pairs)
        dtok = ("d", key_res.dsem, key_res.dcount)
        self.streams[eng].append({"fn": None, "waits": waits, "inc": False,
                                  "dma": (pairs, key_res.dsem)})
        self.snap[eng].append(dict(self.seen[eng]))
        self.dsnap[(key_res.dsem, key_res.dcount)] = dict(self.seen[eng])
        self.dtoks.append(dtok)
        for res in r:
            res.r[("dma", id(key_res))] = dtok
        for res in w:
            res.w = dtok
            res.r = {}
        return dtok

    def wait_only(self, eng, toks):
        waits = []
        for t in toks:
            if t is None:
                continue
            if t[0] == "e":
                if t[1] == eng:
                    continue
                key = ("e", t[1])
            else:
                key = ("d", t[1])
            if self.seen[eng].get(key, -1) >= t[2]:
                continue
            self.seen[eng][key] = t[2]
            waits.append((key, t[2]))
        if waits:
            self.streams[eng].append({"fn": None, "waits": waits, "inc": False, "dma": None})
            self.snap[eng].append(dict(self.seen[eng]))

    def barrier(self):
        toks = [self.last[e] for e in self.ENGS if self.last[e] is not None] + self.dtoks
        for e in self.ENGS:
            self.wait_only(e, toks)
        self.dtoks = []

    def emit(self):
        for e in self.ENGS:
            for o in self.streams[e]:
                for (key, val) in o["waits"]:
                    if key[0] == "e":
                        self.streams[key[1]][val]["inc"] = True
        counts = {}
        for e in self.ENGS:
            c = 0
            cs = []
            for o in self.streams[e]:
                if o["inc"]:
                    c += 1
                cs.append(c)
            counts[e] = cs
            if c > 0:
                self.esem[e] = self._sem("e_" + e)
        ninst = 0
        for e in self.ENGS:
            eo = self.eng[e]
            for o in self.streams[e]:
                waits = []
                for (key, val) in o["waits"]:
                    if key[0] == "e":
                        waits.append((self.esem[key[1]], counts[key[1]][val]))
                    else:
                        waits.append((key[1], val))
                has_ins = (o["dma"] is not None) or (o["fn"] is not None)
                attach = waits.pop() if (has_ins and waits and ATTACH_WAIT) else None
                for (sem, val) in waits:
                    eo.wait_ge(sem, val)
                    ninst += 1
                if o["dma"] is not None:
                    pairs, dsem = o["dma"]
                    for pi, (oa, ia) in enumerate(pairs):
                        ins = eo.dma_start(out=oa, in_=ia)
                        if pi == 0 and attach is not None:
                            ins._wait_ge(attach[0], attach[1])
                        ins.then_inc(dsem, 16)
                        ninst += 1
                elif o["fn"] is not None:
                    ins = o["fn"](eo)
                    if attach is not None:
                        ins._wait_ge(attach[0], attach[1])
                    ninst += 1
                    if o["inc"]:
                        ins.then_inc(self.esem[e], 1)
        return ninst


class Arena:
    def __init__(self, nc, nbytes):
        self.t = nc.alloc_sbuf_tensor("arena", [128, nbytes], U8)
        self.n = nbytes
        self.live = {}

    def alloc(self, name, shape, dtype):
        esz = 4 if dtype == F32 else 2
        nel = 1
        for s in shape[1:]:
            nel *= s
        nb = (nel * esz + 63) // 64 * 64
        segs = sorted(self.live.values())
        off = 0
        for (o, s) in segs:
            if off + nb <= o:
                break
            off = max(off, o + s)
        if off + nb > self.n:
            print("ARENA MAP", sorted((o, sz, n) for n, (o, sz) in self.live.items()))
        assert off + nb <= self.n, ("arena overflow", name, nb, off, self.n)
        assert name not in self.live, name
        self.live[name] = (off, nb)
        ap = self.t[:, off:off + nel * esz].bitcast(dtype)
        if len(shape) == 3:
            ap = ap.rearrange("p (a b) -> p a b", a=shape[1])
        elif len(shape) == 4:
            ap = ap.rearrange("p (a b c) -> p a b c", a=shape[1], b=shape[2])
        return ap

    def free(self, *names):
        for n in names:
            del self.live[n]


def bc(ap, shape):
    return ap.to_broadcast(list(shape))


def _consts():
    p = np.arange(128)
    same = (p[:, None] // 64) == (p[None, :] // 64)
    c = {}
    c["ident"] = np.eye(128, dtype=np.float32)
    c["ones"] = np.ones((128, 128), np.float32)
    c["blk"] = same.astype(np.float32)
    c["tri_f"] = (same & (p[:, None] <= p[None, :])).astype(np.float32)
    c["tri_b"] = (same & (p[:, None] >= p[None, :])).astype(np.float32)
    inc_f = same & (p[None, :] <= p[:, None])
    inc_b = same & (p[None, :] >= p[:, None])
    c["big_f"] = np.where(inc_f, 0.0, BIG).astype(np.float32)
    c["big_b"] = np.where(inc_b, 0.0, BIG).astype(np.float32)
    c["strict_f"] = (same & (p[None, :] < p[:, None])).astype(np.float32)
    c["strict_b"] = (same & (p[None, :] > p[:, None])).astype(np.float32)
    c["mask_p"] = (p[:, None] >= p[None, :]).astype(np.float32)
    c["mask_n"] = (p[:, None] <= p[None, :]).astype(np.float32)
    sm = lambda b: (p[:, None] // b) == (p[None, :] // b)
    c["m8"] = sm(8).astype(np.float32)
    c["l16"] = (sm(16) & ~sm(8)).astype(np.float32)
    c["l32"] = (sm(32) & ~sm(16)).astype(np.float32)
    c["l64"] = (sm(64) & ~sm(32)).astype(np.float32)
    names = ["ident", "ones", "blk", "tri_f", "tri_b", "big_f", "big_b",
             "strict_f", "strict_b", "mask_p", "mask_n", "m8", "l16", "l32", "l64"]
    arr = np.concatenate([c[n] for n in names], axis=1)
    return names, np.ascontiguousarray(arr)


def _rope_tables():
    t = np.arange(SEQ)
    row = (t // 64).astype(np.float32)
    col = (t % 64).astype(np.float32)
    freqs = (10000.0 ** (-np.arange(16, dtype=np.float32) / 16)).astype(np.float32)
    ar = row[:, None] * freqs[None, :]
    ac = col[:, None] * freqs[None, :]
    cos = np.concatenate([np.cos(ar), np.cos(ar), np.cos(ac), np.cos(ac)], axis=1)
    sin = np.concatenate([-np.sin(ar), np.sin(ar), -np.sin(ac), np.sin(ac)], axis=1)
    return cos.astype(np.float32), sin.astype(np.float32)


CONST_NAMES, CONST_ARR = _consts()
ROPE_COS, ROPE_SIN = _rope_tables()

DBG = {}


def build(debug=(), stop_after=None):
    nc = bass.Bass("TRN2", target_bir_lowering=False)
    S = Sched(nc)
    A = Arena(nc, 206 * 1024)

    def din(name, shape):
        return nc.dram_tensor(name, list(shape), F32, kind="ExternalInput").ap()

    x_d = din("x", [SEQ, D])
    ctx_d = din("ctx", [CTX, D])
    cT_d = din("cT", [128, 16])
    badaT_d = din("badaT", [128, 48])
    bada_d = din("bada", [6 * D])
    g1T_d = din("g1T", [128, 8])
    g2T_d = din("g2T", [128, 8])
    wada_d = din("w_ada", [D, 6 * D])
    win_d = din("w_in", [D, IN_W])
    qg_d = din("q_norm_g", [64])
    kg_d = din("k_norm_g", [64])
    sink_d = din("attn_sink", [8])
    convT_d = din("convT", [128, 36])
    alog_d = din("a_log", [16])
    dtb_d = din("dt_bias", [16])
    dng_d = din("dn_norm_g", [64])
    wbra_d = din("w_br_attn", [512, D])
    wbrd_d = din("w_br_dn", [512, D])
    wout_d = din("w_out", [D, D])
    w1_d = din("w_mlp1", [D, 4 * D])
    w2_d = din("w_mlp2", [4 * D, D])
    const_d = din("consts", list(CONST_ARR.shape))
    cos_d = din("rope_cos", [SEQ, 64])
    sin_d = din("rope_sin", [SEQ, 64])
    out_d = nc.dram_tensor("out", [SEQ, D], F32, kind="ExternalOutput").ap()

    dbg_out = {}

    def dump(name, ap, res):
        if name not in debug:
            return
        shape = list(ap.shape)
        d = nc.dram_tensor("dbg_" + name, shape, F32, kind="ExternalOutput").ap()
        dbg_out[name] = shape
        S.dma("pool", [(d, ap)], r=[res])

    banks = []
    for i in range(8):
        t = nc.alloc_psum_tensor("pb%d" % i, [128, 512], F32)
        banks.append((t[:], t[:].bitcast(BF16), Res("pb%d" % i, excl=True)))
    bank_i = [0]

    def nb():
        b = banks[bank_i[0] % 8]
        bank_i[0] += 1
        return b

    def mm(out, lhsT, rhs, start=True, stop=True, r=(), w=(), skip=False):
        if skip:
            S.op("pe", lambda e: e.matmul(out, lhsT, rhs, start=start, stop=stop, skip_group_check=True), r=r, w=w)
        else:
            S.op("pe", lambda e: e.matmul(out, lhsT, rhs, start=start, stop=stop), r=r, w=w)

    def tr(out, in_, ident, r=(), w=()):
        S.op("pe", lambda e: e.transpose(out, in_, ident), r=r, w=w)

    def act(out, in_, func, r=(), w=(), scale=1.0, bias=None, accum=None):
        def f(e):
            kw = {"scale": scale}
            if bias is not None:
                kw["bias"] = bias
            if accum is not None:
                kw["accum_out"] = accum
            return e.activation(out, in_, func, **kw)
        S.op("act", f, r=r, w=w)

    def tt(eng, out, in0, in1, op, r=(), w=()):
        S.op(eng, lambda e: e.tensor_tensor(out, in0, in1, op), r=r, w=w)

    def ts(eng, out, in0, s1, op0, s2=None, op1=None, r=(), w=()):
        if op1 is None:
            S.op(eng, lambda e: e.tensor_scalar(out, in0, s1, None, op0), r=r, w=w)
        else:
            S.op(eng, lambda e: e.tensor_scalar(out, in0, s1, s2, op0, op1), r=r, w=w)

    def stt(eng, out, in0, scalar, in1, op0, op1, r=(), w=()):
        S.op(eng, lambda e: e.scalar_tensor_tensor(out, in0, scalar, in1, op0, op1), r=r, w=w)

    def cp(eng, out, in_, r=(), w=()):
        if eng == "act":
            S.op("act", lambda e: e.activation(out, in_, AF.Copy), r=r, w=w)
        else:
            S.op(eng, lambda e: e.tensor_copy(out, in_), r=r, w=w)

    def recip(out, in_, r=(), w=()):
        S.op("dve", lambda e: e.reciprocal(out, in_), r=r, w=w)

    def memset(eng, ap, val, w=()):
        S.op(eng, lambda e: e.memset(ap, val), w=w)

    def rstd_from_ss(out, ss, inv_n, tmp, r_ss, r_tmp, r_out):
        ts("dve", tmp, ss, inv_n, ALU.mult, EPS, ALU.add, r=[r_ss], w=[r_tmp])
        act(tmp, tmp, AF.Ln, r=[r_tmp], w=[r_tmp])
        act(out, tmp, AF.Exp, scale=-0.5, r=[r_tmp], w=[r_out])

    NC_ = CONST_ARR.shape[1]
    cidx = {n: i for i, n in enumerate(CONST_NAMES)}

    cb = A.alloc("constb", [128, NC_], BF16)
    r_cb = Res("constb")
    S.dma("pool", [(cb, const_d)], w=[r_cb])

    def cB(name):
        i = cidx[name]
        return cb[:, i * 128:(i + 1) * 128]

    small = A.alloc("small", [128, 512], F32)
    r_small = Res("small")
    so = [0]

    def salloc(n):
        o = so[0]
        so[0] += n
        assert so[0] <= 512
        return small[:, o:o + n]

    cT = salloc(16)
    badaT = salloc(48)
    g1T = salloc(8)
    g2T = salloc(8)
    convT = salloc(36)
    qg_b = salloc(64)
    kg_b = salloc(64)
    dng_b = salloc(64)
    sink_b = salloc(8)
    alog_b = salloc(16)
    dtb_b = salloc(16)
    S.dma("sp", [(cT, cT_d), (badaT, badaT_d), (g1T, g1T_d), (g2T, g2T_d), (convT, convT_d),
                 (qg_b, qg_d.partition_broadcast(128)), (kg_b, kg_d.partition_broadcast(128)),
                 (dng_b, dng_d.partition_broadcast(128)), (sink_b, sink_d.partition_broadcast(128)),
                 (alog_b, alog_d.partition_broadcast(128)), (dtb_b, dtb_d.partition_broadcast(128))],
          w=[r_small])
    esink = salloc(8)
    negA = salloc(16)
    act(esink, sink_b, AF.Exp, r=[r_small], w=[r_small])
    act(negA, alog_b, AF.Exp, r=[r_small], w=[r_small])
    ts("dve", negA, negA, -1.0, ALU.mult, r=[r_small], w=[r_small])

    gtrow = A.alloc("gtrow", [128, 2, D], F32)
    r_gtrow = Res("gtrow")
    S.dma("sp", [(gtrow[:, 0, :], bada_d[2 * D:3 * D].partition_broadcast(128)),
                 (gtrow[:, 1, :], bada_d[5 * D:6 * D].partition_broadcast(128))], w=[r_gtrow])

    modv = A.alloc("modv", [128, 48, 2], F32)
    r_modv = Res("modv")
    coef = A.alloc("coef", [128, 4, 8], F32)
    r_coef = Res("coef")

    scT = A.alloc("scT", [128, 16], BF16)
    screp = A.alloc("screp", [128, 8, 128], BF16)
    r_sc = Res("sc")
    tmp16 = A.alloc("tmp16", [128, 16], F32)
    r_tmp16 = Res("tmp16")
    act(tmp16, cT, AF.Exp, scale=-1.0, r=[r_small], w=[r_tmp16])
    ts("dve", tmp16, tmp16, 1.0, ALU.add, r=[r_tmp16], w=[r_tmp16])
    recip(tmp16, tmp16, r=[r_tmp16], w=[r_tmp16])
    tt("dve", scT, cT, tmp16, ALU.mult, r=[r_small, r_tmp16], w=[r_sc])
    cp("dve", screp, bc(scT.rearrange("p (k t) -> p k t", t=2)[:, :, 0:1], [128, 8, 128]), r=[r_sc], w=[r_sc])

    wada_v = wada_d.rearrange("(k p) n -> p k n", p=128)
    wab = [A.alloc("wada%d" % i, [128, 8, 512], BF16) for i in range(2)]
    r_wab = [Res("wada%d" % i) for i in range(2)]
    psm, _, r_psm = nb()
    for j in range(12):
        wa, r_wa = wab[j % 2], r_wab[j % 2]
        S.dma("pool", [(wa, wada_v[:, :, j * 512:(j + 1) * 512])], w=[r_wa])
        for m in range(4):
            cidx_ = j * 4 + m
            for k in range(8):
                mm(psm[:, cidx_ * 2:cidx_ * 2 + 2], wa[:, k, m * 128:(m + 1) * 128], scT[:, 2 * k:2 * k + 2],
                   start=(k == 0), stop=(k == 7), r=[r_wa, r_sc], w=[r_psm])
        if j in (4, 5, 10, 11):
            which = 0 if j < 6 else 1
            half = j % 2
            psr, _, r_psr = nb()
            for k in range(8):
                mm(psr[:, 0:512], screp[:, k, :], wa[:, k, :], start=(k == 0), stop=(k == 7),
                   r=[r_wa, r_sc], w=[r_psr])
            dst = gtrow[:, which, half * 512:(half + 1) * 512]
            tt("dve", dst, psr[:, 0:512], dst, ALU.add, r=[r_psr, r_gtrow], w=[r_gtrow])
    tt("dve", modv, psm[:, 0:96].rearrange("p (c t) -> p c t", t=2), bc(badaT.unsqueeze(2), [128, 48, 2]),
       ALU.add, r=[r_psm, r_small], w=[r_modv])
    stt("dve", coef[:, 0, :], modv[:, 8:16, 0], 1.0, g1T, ALU.add, ALU.mult, r=[r_modv, r_small], w=[r_coef])
    stt("dve", coef[:, 1, :], modv[:, 8:16, 1], 1.0, g1T, ALU.add, ALU.mult, r=[r_modv, r_small], w=[r_coef])
    stt("dve", coef[:, 2, :], modv[:, 32:40, 0], 1.0, g2T, ALU.add, ALU.mult, r=[r_modv, r_small], w=[r_coef])
    dump("modv", modv, r_modv)
    dump("gtrow", gtrow, r_gtrow)
    A.free("wada0", "wada1", "scT", "screp", "tmp16")

    hT = A.alloc("hT", [128, 8, TOK], BF16)
    r_hT = [Res("hT%d" % t) for t in range(NT)]
    stat = A.alloc("stat", [128, 64], F32)
    r_stat = Res("stat")

    def nt_front(src_ap, r_src, bufs, i):
        junk, r_junk, xn, r_xn = bufs
        ss = stat[:, (i % 8) * 4:(i % 8) * 4 + 1]
        tm = stat[:, (i % 8) * 4 + 1:(i % 8) * 4 + 2]
        rs = stat[:, (i % 8) * 4 + 2:(i % 8) * 4 + 3]
        memset("dve", ss, 0.0, w=[r_stat])
        act(junk, src_ap, AF.Square, accum=ss, r=[r_src, r_stat], w=[r_junk, r_stat])
        rstd_from_ss(rs, ss, 1.0 / D, tm, r_stat, r_stat, r_stat)
        ts("dve", xn, src_ap, rs, ALU.mult, r=[r_src, r_stat], w=[r_xn])

    def nt_back(dstT, tokslice, r_dst, Acoef, Bcoef, r_ab, bufs, i):
        junk, r_junk, xn, r_xn = bufs
        _, pbb, r_pb = nb()
        for c in range(8):
            tr(pbb[:, c * 128:(c + 1) * 128], xn[:, c * 128:(c + 1) * 128], cB("ident"),
               r=[r_xn, r_cb], w=[r_pb])
        for c in range(8):
            dst = dstT[:, c, tokslice]
            if i % 2 == 0:
                act(dst, pbb[:, c * 128:(c + 1) * 128], AF.Identity, scale=Acoef[:, c:c + 1], bias=Bcoef(c),
                    r=[r_pb] + r_ab, w=[r_dst])
            else:
                ts("dve", dst, pbb[:, c * 128:(c + 1) * 128], Acoef[:, c:c + 1], ALU.mult, Bcoef(c), ALU.add,
                   r=[r_pb] + r_ab, w=[r_dst])

    def norm_transpose_seq(items):
        for k, it in enumerate(items):
            if k == 0:
                if it.get("pre"):
                    it["pre"]()
                nt_front(it["src"], it["r_src"], it["bufs"], it["i"])
            if k + 1 < len(items):
                nx = items[k + 1]
                if nx.get("pre"):
                    nx["pre"]()
                nt_front(nx["src"], nx["r_src"], nx["bufs"], nx["i"])
            nt_back(it["dstT"], it["tok"], it["r_dst"], it["A"], it["B"], [r_coef, r_modv], it["bufs"], it["i"])

    xbuf = [A.alloc("xbuf%d" % i, [128, D], F32) for i in range(2)]
    r_xbuf = [Res("xbuf%d" % i) for i in range(2)]
    junk = A.alloc("junk", [128, D], BF16)
    r_junk = Res("junk")
    xnb = [A.alloc("xn%d" % i, [128, D], BF16) for i in range(2)]
    r_xnb = [Res("xn%d" % i) for i in range(2)]
    items = []
    for t in range(NT):
        src = ctx_d[t * 128:(t + 1) * 128, :] if t < 2 else x_d[(t - 2) * 128:(t - 1) * 128, :]
        xb, r_xb = xbuf[t % 2], r_xbuf[t % 2]
        if t < 2:
            Ac = coef[:, 1, :]
            Bc = (lambda c: modv[:, c, 1:2])
        else:
            Ac = coef[:, 0, :]
            Bc = (lambda c: modv[:, c, 0:1])
        items.append(dict(src=xb, r_src=r_xb, dstT=hT, tok=slice(t * 128, (t + 1) * 128), r_dst=r_hT[t], A=Ac, B=Bc,
                          bufs=(junk, r_junk, xnb[t % 2], r_xnb[t % 2]), i=t,
                          pre=(lambda xb=xb, src=src, r_xb=r_xb: S.dma("sp", [(xb, src)], w=[r_xb]))))
    norm_transpose_seq(items)
    if "hT" in debug:
        r_all = Res("hTall")
        S.wait_only("pool", [r.w for r in r_hT])
        dump("hT", hT, r_all)
    A.free("xbuf0", "xbuf1", "junk", "xn0", "xn1")
    if stop_after == 1:
        return finish(nc, S, out_d, dbg_out, None)
    S.barrier()

    win_v = win_d.rearrange("(k p) n -> p k n", p=128)
    wq = A.alloc("wq", [128, 8, 512], BF16)
    wkv = A.alloc("wkv", [128, 8, 288], BF16)
    r_wq, r_wkv = Res("wq"), Res("wkv")
    S.dma("pool", [(wq, win_v[:, :, 0:512])], w=[r_wq])
    S.dma("pool", [(wkv[:, :, 0:256], win_v[:, :, 512:768]), (wkv[:, :, 256:288], win_v[:, :, 2816:2848])], w=[r_wkv])
    cosT = A.alloc("cosT", [128, 16, 64], F32)
    sinT = A.alloc("sinT", [128, 16, 64], F32)
    r_rope = Res("rope")
    S.dma("sp", [(cosT, cos_d.rearrange("(t p) d -> p t d", p=128)),
                 (sinT, sin_d.rearrange("(t p) d -> p t d", p=128))], w=[r_rope])
    qT = A.alloc("qT", [128, 8, SEQ], BF16)
    kT = A.alloc("kT", [128, 2, TOK], BF16)
    vA = A.alloc("vA", [128, NT, 2, 65], BF16)
    r_qT = [Res("qT%d" % t) for t in range(16)]
    r_kT = [Res("kT%d" % t) for t in range(NT)]
    r_vA = [Res("vA%d" % t) for t in range(NT)]
    gall = A.alloc("gall", [128, NT, 16], F32)
    ball = A.alloc("ball", [128, NT, 16], F32)
    r_g = [Res("g%d" % t) for t in range(NT)]
    memset("pool", vA[:, :, :, 64:65], 1.0, w=r_vA)

    wk2 = [{}, {}]
    for nm, shp, dt in [("qsq", [128, 512], F32), ("qn", [128, 8, 64], F32), ("qg", [128, 8, 64], F32),
                        ("qt", [128, 8, 64], F32), ("qu", [128, 8, 64], F32), ("qr", [128, 512], BF16),
                        ("ksq", [128, 128], F32), ("kn", [128, 2, 64], F32), ("kg", [128, 2, 64], F32),
                        ("kt", [128, 2, 64], F32), ("ku", [128, 2, 64], F32), ("kr", [128, 128], BF16),
                        ("gt", [128, 64], F32)]:
        for pp in range(2):
            wk2[pp][nm] = (A.alloc("w%d_%s" % (pp, nm), shp, dt), Res("w%d_%s" % (pp, nm)))
    wk = wk2[0]

    def qk_norm_rope(ps_ap, r_ps, nh, gain_b, tile_lat, pre, wk):
        sq, r_sq = wk[pre + "sq"]
        xn, r_xn = wk[pre + "n"]
        xg, r_xg = wk[pre + "g"]
        xt_, r_xt = wk[pre + "t"]
        xu, r_xu = wk[pre + "u"]
        xr, r_xr = wk[pre + "r"]
        gt_, r_gt = wk["gt"]
        W = nh * 64
        ps3 = ps_ap.rearrange("p (h d) -> p h d", h=nh)
        act(sq[:, 0:W], ps_ap, AF.Square, r=[r_ps], w=[r_sq])
        ss = gt_[:, 0:nh]
        tm = gt_[:, 8:8 + nh]
        rs = gt_[:, 16:16 + nh]
        S.op("dve", lambda e: e.tensor_reduce(ss, sq[:, 0:W].rearrange("p (h d) -> p h d", h=nh), AX.X, ALU.add),
             r=[r_sq], w=[r_gt])
        rstd_from_ss(rs, ss, 1.0 / 64, tm, r_gt, r_gt, r_gt)
        tt("dve", xn, ps3, bc(rs.unsqueeze(2), [128, nh, 64]), ALU.mult, r=[r_ps, r_gt], w=[r_xn])
        if tile_lat is None:
            tt("pool", xr.rearrange("p (h d) -> p h d", h=nh), xn, bc(gain_b.unsqueeze(1), [128, nh, 64]), ALU.mult,
               r=[r_xn, r_small], w=[r_xr])
            return xr, r_xr
        tt("pool", xg, xn, bc(gain_b.unsqueeze(1), [128, nh, 64]), ALU.mult, r=[r_xn, r_small], w=[r_xg])
        cs = cosT[:, tile_lat, :]
        sn = sinT[:, tile_lat, :]
        tt("pool", xt_, xg, bc(cs.unsqueeze(1), [128, nh, 64]), ALU.mult, r=[r_xg, r_rope], w=[r_xt])
        xg5 = xg.rearrange("p h (a b c) -> p h a b c", a=2, b=2)
        xu5 = xu.rearrange("p h (a b c) -> p h a b c", a=2, b=2)
        sn4 = sn.rearrange("p (a b c) -> p a b c", a=2, b=2)
        for bsel in range(2):
            tt("dve", xu5[:, :, :, bsel, :], xg5[:, :, :, 1 - bsel, :],
               bc(sn4[:, :, bsel, :].unsqueeze(1), [128, nh, 2, 16]), ALU.mult, r=[r_xg, r_rope], w=[r_xu])
        tt("pool", xr.rearrange("p (h d) -> p h d", h=nh), xt_, xu, ALU.add, r=[r_xt, r_xu], w=[r_xr])
        return xr, r_xr

    tstate = {}

    def p2_proj(t):
        lat = t - 2 if t >= 2 else None
        psq, _, r_psq = nb()
        pskv, _, r_pskv = nb()
        if lat is not None:
            for k in range(8):
                mm(psq[:, 0:512], hT[:, k, t * 128:(t + 1) * 128], wq[:, k, :], start=(k == 0), stop=(k == 7),
                   r=[r_hT[t], r_wq], w=[r_psq])
        for k in range(8):
            mm(pskv[:, 0:288], hT[:, k, t * 128:(t + 1) * 128], wkv[:, k, :], start=(k == 0), stop=(k == 7),
               r=[r_hT[t], r_wkv], w=[r_pskv])
        tstate[t] = {"psq": (psq, r_psq), "pskv": (pskv, r_pskv)}

    def p2_mid(t):
        lat = t - 2 if t >= 2 else None
        wk = wk2[t % 2]
        psq, r_psq = tstate[t]["psq"]
        pskv, r_pskv = tstate[t]["pskv"]
        cp("act", vA[:, t, :, 0:64], pskv[:, 128:256].rearrange("p (h d) -> p h d", h=2), r=[r_pskv], w=[r_vA[t]])
        gt_, r_gt = wk["gt"]
        xa = gt_[:, 24:40]
        xb_ = gt_[:, 40:56]
        tt("dve", xa, pskv[:, 256:272], dtb_b, ALU.add, r=[r_pskv, r_small], w=[r_gt])
        act(xa, xa, AF.Exp, r=[r_gt], w=[r_gt])
        ts("dve", xa, xa, 1.0, ALU.add, r=[r_gt], w=[r_gt])
        act(xa, xa, AF.Ln, r=[r_gt], w=[r_gt])
        tt("dve", gall[:, t, :], xa, negA, ALU.mult, r=[r_gt, r_small], w=[r_g[t]])
        act(xb_, pskv[:, 272:288], AF.Exp, scale=-1.0, r=[r_pskv], w=[r_gt])
        ts("dve", xb_, xb_, 1.0, ALU.add, r=[r_gt], w=[r_gt])
        recip(ball[:, t, :], xb_, r=[r_gt], w=[r_g[t]])
        tstate[t]["kr"] = qk_norm_rope(pskv[:, 0:128], r_pskv, 2, kg_b, lat, "k", wk)
        if lat is not None:
            tstate[t]["qr"] = qk_norm_rope(psq[:, 0:512], r_psq, 8, qg_b, lat, "q", wk)

    def p2_back(t):
        lat = t - 2 if t >= 2 else None
        kr, r_kr = tstate[t]["kr"]
        _, pbb, r_pb = nb()
        for g in range(2):
            tr(pbb[0:64, g * 128:(g + 1) * 128], kr[:, g * 64:(g + 1) * 64], cB("ident"), r=[r_kr, r_cb], w=[r_pb])
        cp("dve", kT[0:64, :, t * 128:(t + 1) * 128], pbb[0:64, 0:256].rearrange("p (g t) -> p g t", g=2),
           r=[r_pb], w=[r_kT[t]])
        if lat is not None:
            qr, r_qr = tstate[t]["qr"]
            _, pbb2, r_pb2 = nb()
            for h in range(8):
                tr(pbb2[0:64, h * 128:(h + 1) * 128], qr[:, h * 64:(h + 1) * 64], cB("ident"), r=[r_qr, r_cb], w=[r_pb2])
            cp("act", qT[0:64, :, lat * 128:(lat + 1) * 128], pbb2[0:64, :].rearrange("p (h t) -> p h t", h=8),
               r=[r_pb2], w=[r_qT[lat]])

    p2_proj(0)
    p2_proj(1)
    p2_mid(0)
    for t in range(NT):
        if t + 2 < NT:
            p2_proj(t + 2)
        if t + 1 < NT:
            p2_mid(t + 1)
        p2_back(t)
    gp = [A.alloc("gp%d" % i, [128, NT, 16], BF16) for i in range(3)]
    r_gp = Res("gp")
    gr = [A.alloc("gr%d" % i, [128, NT, 16], F32) for i in range(2)]
    r_gr = Res("gr")
    cp("dve", gp[0], gall, r=r_g, w=[r_gp])
    tt("dve", gr[0], gall, gp[0], ALU.subtract, r=r_g + [r_gp], w=[r_gr])
    cp("dve", gp[1], gr[0], r=[r_gr], w=[r_gp])
    tt("dve", gr[1], gr[0], gp[1], ALU.subtract, r=[r_gr, r_gp], w=[r_gr])
    cp("dve", gp[2], gr[1], r=[r_gr], w=[r_gp])
    A.free("gr0", "gr1")
    if "qT" in debug:
        r_all = Res("dall")
        S.wait_only("pool", [r.w for r in r_qT] + [r.w for r in r_kT] + [r.w for r in r_vA] + [r.w for r in r_g])
        dump("qT", qT[0:64], r_all)
        dump("kT", kT[0:64], r_all)
        dump("vA", vA, r_all)
        dump("gall", gall, r_all)
        dump("ball", ball, r_all)
    for pp in range(2):
        for nm in list(wk2[pp].keys()):
            A.free("w%d_%s" % (pp, nm))
    A.free("wq", "wkv", "cosT", "sinT")
    if stop_after == 2:
        return finish(nc, S, out_d, dbg_out, None)
    S.barrier()

    yattnT = A.alloc("yattnT", [128, 4, SEQ], BF16)
    r_yat = [Res("yat%d" % t) for t in range(16)]
    ptb = [A.alloc("pt%d" % i, [128, 512], BF16) for i in range(3)]
    r_ptb = [Res("pt%d" % i) for i in range(3)]
    ytile = [A.alloc("ytile%d" % i, [128, 512], BF16) for i in range(2)]
    r_ytile = [Res("ytile%d" % i) for i in range(2)]
    den = A.alloc("den", [128, 16], F32)
    r_den = Res("den")
    obanks = [banks[0], banks[1]]
    sbanks = [banks[2], banks[3], banks[4]]
    tbanks = [banks[5], banks[6]]
    its = []
    for n in range(16):
        for g in range(2):
            kbs = []
            if n > 0:
                kbs.append((n + 1, "mask_p"))
            kbs.append((n + 2, None))
            if n < 15:
                kbs.append((n + 3, "mask_n"))
            kbs.append((0, None))
            kbs.append((1, None))
            for idx, (kt_i, mk) in enumerate(kbs):
                its.append((n, g, idx, len(kbs), kt_i, mk))

    def emit_qk(ii):
        n, g, idx, nk, kt_i, mk = its[ii]
        pss, _, r_pss = sbanks[ii % 3]
        mm(pss[:, 0:512], kT[0:64, g, kt_i * 128:(kt_i + 1) * 128],
           qT[0:64, 4 * g:4 * g + 4, n * 128:(n + 1) * 128],
           r=[r_kT[kt_i], r_qT[n]], w=[r_pss])

    emit_qk(0)
    emit_qk(1)
    pend_tr = []
    for ii, (n, g, idx, nk, kt_i, mk) in enumerate(its):
        yt_, r_yt = ytile[n % 2], r_ytile[n % 2]
        pso, _, r_pso = obanks[g]
        pss, _, r_pss = sbanks[ii % 3]
        if ii + 2 < len(its):
            emit_qk(ii + 2)
        pt, r_pt = ptb[ii % 3], r_ptb[ii % 3]
        act(pt, pss[:, 0:512], AF.Exp, scale=0.125, r=[r_pss], w=[r_pt])
        if mk is not None:
            tt("dve", pt.rearrange("p (h q) -> p h q", h=4), pt.rearrange("p (h q) -> p h q", h=4),
               bc(cB(mk).unsqueeze(1), [128, 4, 128]), ALU.mult, r=[r_pt, r_cb], w=[r_pt])
        for rr in range(4):
            mm(pso[:, rr * 65:(rr + 1) * 65], pt[:, rr * 128:(rr + 1) * 128], vA[:, kt_i, g, :],
               start=(idx == 0 and rr == 0), stop=(idx == nk - 1), r=[r_pt, r_vA[kt_i]], w=[r_pso],
               skip=True)
        if idx == nk - 1:
            pso3 = pso[:, 0:260].rearrange("p (h d) -> p h d", h=4)
            dn_ = den[:, g * 4:(g + 1) * 4]
            tt("dve", dn_.unsqueeze(2), pso3[:, :, 64:65], esink[:, 4 * g:4 * g + 4].unsqueeze(2), ALU.add,
               r=[r_pso, r_small], w=[r_den])
            recip(dn_, dn_, r=[r_den], w=[r_den])
            tt("dve", yt_[:, g * 256:(g + 1) * 256].rearrange("p (h d) -> p h d", h=4), pso3[:, :, 0:64],
               bc(dn_.unsqueeze(2), [128, 4, 64]), ALU.mult, r=[r_pso, r_den], w=[r_yt])
            if g == 1:
                pend_tr.append((ii + 3, n))
        while pend_tr and (pend_tr[0][0] <= ii or ii == len(its) - 1):
            _, n_ = pend_tr.pop(0)
            ytn, r_ytn = ytile[n_ % 2], r_ytile[n_ % 2]
            _, pbb, r_pb = tbanks[n_ % 2]
            for c in range(4):
                tr(pbb[:, c * 128:(c + 1) * 128], ytn[:, c * 128:(c + 1) * 128], cB("ident"), r=[r_ytn, r_cb], w=[r_pb])
            cp("act", yattnT[:, :, n_ * 128:(n_ + 1) * 128], pbb[:, 0:512].rearrange("p (c t) -> p c t", c=4),
               r=[r_pb], w=[r_yat[n_]])
    if "yattnT" in debug:
        r_all = Res("dall2")
        S.wait_only("pool", [r.w for r in r_yat])
        dump("yattnT", yattnT, r_all)
    A.free("pt0", "pt1", "pt2", "ytile0", "ytile1", "den", "qT", "kT", "vA")
    if stop_after == 3:
        return finish(nc, S, out_d, dbg_out, None)
    S.barrier()

    pre = A.alloc("pre", [128, 12, TOK], BF16)
    r_pre = [Res("pre%d" % c) for c in range(12)]
    wd = [A.alloc("wd%d" % i, [128, 8, 512], BF16) for i in range(2)]
    r_wd = [Res("wd%d" % i) for i in range(2)]
    accb = [A.alloc("acc%d" % i, [128, TOK], F32) for i in range(2)]
    r_accb = [Res("acc%d" % i) for i in range(2)]
    seqs = [(0, 256), (256, TOK)]

    def conv_silu(c):
        acc, r_acc = accb[c % 2], r_accb[c % 2]
        w0 = convT[:, c * 3 + 0:c * 3 + 1]
        w1c = convT[:, c * 3 + 1:c * 3 + 2]
        w2c = convT[:, c * 3 + 2:c * 3 + 3]
        ts("dve", acc, pre[:, c, :], w1c, ALU.mult, r=[r_pre[c], r_small], w=[r_acc])
        for (s0, e0) in seqs:
            stt("dve", acc[:, s0 + 1:e0], pre[:, c, s0:e0 - 1], w0, acc[:, s0 + 1:e0], ALU.mult, ALU.add,
                r=[r_pre[c], r_small, r_acc], w=[r_acc])
        for (s0, e0) in seqs:
            stt("dve", acc[:, s0:e0 - 1], pre[:, c, s0 + 1:e0], w2c, acc[:, s0:e0 - 1], ALU.mult, ALU.add,
                r=[r_pre[c], r_small, r_acc], w=[r_acc])
        act(pre[:, c, :], acc, AF.Silu, r=[r_acc], w=[r_pre[c]])

    tokblocks = [(0, 256)] + [(256 + 512 * i, 512) for i in range(4)]
    ev = 0
    for j in range(3):
        wdj, r_wdj = wd[j % 2], r_wd[j % 2]
        S.dma("pool", [(wdj, win_v[:, :, 768 + 512 * j:768 + 512 * (j + 1)])], w=[r_wdj])
        for (t0, n) in tokblocks:
            rts = [r_hT[t] for t in range(t0 // 128, (t0 + n) // 128)]
            for cc in range(4):
                c = 4 * j + cc
                ps, _, r_ps = nb()
                for k in range(8):
                    mm(ps[:, 0:n], wdj[:, k, cc * 128:(cc + 1) * 128], hT[:, k, t0:t0 + n], start=(k == 0), stop=(k == 7),
                       r=[r_wdj] + rts, w=[r_ps])
                cp("act", pre[:, c, t0:t0 + n], ps[:, 0:n], r=[r_ps], w=[r_pre[c]])
                ev += 1
        if j > 0:
            for c in range(4 * (j - 1), 4 * j):
                conv_silu(c)
    for c in range(8, 12):
        conv_silu(c)
    A.free("wd0", "wd1")
    A.free("acc0", "acc1")
    sqb = [A.alloc("sqb%d" % i, [128, TOK], BF16) for i in range(2)]
    r_sqb = [Res("sqb%d" % i) for i in range(2)]
    rnb = [A.alloc("rnb%d" % i, [128, TOK], F32) for i in range(2)]
    r_rnb = [Res("rnb%d" % i) for i in range(2)]
    for c in range(8):
        sq_, r_sq_ = sqb[c % 2], r_sqb[c % 2]
        rn, r_rn = rnb[c % 2], r_rnb[c % 2]
        act(sq_, pre[:, c, :], AF.Square, r=[r_pre[c]], w=[r_sq_])
        for (t0, n) in tokblocks:
            ps, _, r_ps = nb()
            mm(ps[:, 0:n], cB("blk"), sq_[:, t0:t0 + n], r=[r_cb, r_sq_], w=[r_ps])
            ts("dve", rn[:, t0:t0 + n], ps[:, 0:n], EPS, ALU.add, 64.0 if c < 4 else 1.0, ALU.mult, r=[r_ps], w=[r_rn])
        act(rn, rn, AF.Ln, r=[r_rn], w=[r_rn])
        act(rn, rn, AF.Exp, scale=-0.5, r=[r_rn], w=[r_rn])
        tt("pool", pre[:, c, :], pre[:, c, :], rn, ALU.mult, r=[r_pre[c], r_rn], w=[r_pre[c]])
    if "pre" in debug:
        r_all = Res("dall3")
        S.wait_only("pool", [r.w for r in r_pre])
        dump("pre", pre, r_all)
    A.free("sqb0", "sqb1", "rnb0", "rnb1")
    if stop_after == 4:
        return finish(nc, S, out_d, dbg_out, None)
    S.barrier()
    A.free("hT")

    o_d = A.alloc("o_d", [128, 16, 512], BF16)
    r_od = [Res("od%d" % t) for t in range(16)]
    matsD, halfbD, tokbD, st5D = [], [], [], []
    for d in range(2):
        mats = {}
        for nm in ["C", "B", "intra"] + ["m%d" % i for i in range(8)]:
            mats[nm] = (A.alloc("m%d_%s" % (d, nm), [128, 8, 128], BF16), Res("m%d_%s" % (d, nm)))
        halfb = {}
        for nm in ["E", "kkS", "nbE"]:
            halfb[nm] = (A.alloc("h%d_%s" % (d, nm), [128, 4, 128], F32), Res("h%d_%s" % (d, nm)))
        tokb = {}
        for nm in ["qd", "kbg", "vb"]:
            tokb[nm] = (A.alloc("t%d_%s" % (d, nm), [128, 512], BF16), Res("t%d_%s" % (d, nm)))
        matsD.append(mats)
        halfbD.append(halfb)
        tokbD.append(tokb)
        st5D.append((A.alloc("st5_%d" % d, [128, 64], F32), Res("st5_%d" % d)))
    per = []
    for d in range(2):
        pd = {}
        for nm, shp, dt in [("WT", [128, 8, 128], BF16), ("U", [128, 512], F32), ("QdT0", [128, 8, 128], BF16),
                            ("QdT1", [128, 8, 128], BF16), ("intraT", [128, 8, 128], BF16), ("ktl0", [128, 512], BF16),
                            ("ktl1", [128, 512], BF16), ("egl0", [128, 16], F32), ("egl1", [128, 16], F32),
                            ("S", [128, 512], F32), ("Sbf", [128, 512], BF16), ("ubf", [128, 512], BF16),
                            ("tmpS", [128, 512], F32)]:
            pd[nm] = (A.alloc("d%d_%s" % (d, nm), shp, dt), Res("d%d_%s" % (d, nm)))
        per.append(pd)
        memset("pool", pd["S"][0], 0.0, w=[pd["S"][1]])
        memset("pool", pd["Sbf"][0], 0.0, w=[pd["Sbf"][1]])
        memset("pool", pd["ubf"][0], 0.0, w=[pd["ubf"][1]])
    mres = [[[Res("mh%d_%d_%d" % (d, i, hf)) for hf in range(2)] for i in range(8)] for d in range(2)]
    evc = [0]

    def evac(out, in_, r, w):
        cp("dve" if evc[0] % 4 == 3 else "act", out, in_, r=r, w=w)
        evc[0] += 1

    od_seen = set()
    CUT = float(os.environ.get("DN_CUT", 99))

    def dn_visit(t, d, par):
        sfx = "f" if d == 0 else "b"
        lat = t >= 2
        pd = per[d]
        mats, halfb, tokb = matsD[d], halfbD[d], tokbD[d]
        st5, r_st5 = st5D[d]
        tsl = slice(t * 128, (t + 1) * 128)
        gd_ = gall[:, t, d * 8:(d + 1) * 8]
        bd_ = ball[:, t, d * 8:(d + 1) * 8]
        Gc, eG, bEG, ktw, nbeta = [st5[:, i * 8:(i + 1) * 8] for i in range(5)]
        egl, r_egl = pd["egl%d" % par]
        psg, _, r_psg = nb()
        for (dst, lh) in [(psg[:, 0:8], cB("tri_" + sfx)), (psg[:, 8:16], cB("blk")),
                          (psg[0:64, 16:24], cB("blk")[:, 0:64]), (psg[0:64, 24:32], cB("blk")[:, 64:128])]:
            for pc in range(3):
                mm(dst, lh, gp[pc][:, t, d * 8:(d + 1) * 8], start=(pc == 0), stop=(pc == 2),
                   r=[r_cb, r_gp], w=[r_psg])
        cp("dve", Gc, psg[:, 0:8], r=[r_psg], w=[r_st5])
        act(eG, Gc, AF.Exp, r=[r_st5], w=[r_st5])
        tt("dve", bEG, bd_, eG, ALU.mult, r=[r_g[t], r_st5], w=[r_st5])
        tt("dve", ktw, psg[:, 8:16], Gc, ALU.subtract, r=[r_psg, r_st5], w=[r_st5])
        act(ktw, ktw, AF.Exp, r=[r_st5], w=[r_st5])
        ts("dve", nbeta, bd_, -1.0, ALU.mult, r=[r_g[t]], w=[r_st5])
        act(egl[0:64, 0:16], psg[0:64, 16:32], AF.Exp, r=[r_psg], w=[r_egl])
        if CUT <= 1:
            return
        yield
        _, pqk, r_pqk = nb()
        _, pv, r_pv = nb()
        for c in range(4 if lat else 0):
            tr(pqk[:, c * 128:(c + 1) * 128], pre[:, c, tsl], cB("ident"), r=[r_pre[c], r_cb], w=[r_pqk])
        for c in range(4, 8):
            tr(pqk[:, c * 128:(c + 1) * 128], pre[:, c, tsl], cB("ident"), r=[r_pre[c], r_cb], w=[r_pqk])
        for c in range(8, 12):
            tr(pv[:, (c - 8) * 128:(c - 7) * 128], pre[:, c, tsl], cB("ident"), r=[r_pre[c], r_cb], w=[r_pv])
        qd, r_qd = tokb["qd"]
        kbg, r_kbg = tokb["kbg"]
        vb, r_vb = tokb["vb"]
        ktl, r_ktl = pd["ktl%d" % par]
        h3 = lambda ap: ap.rearrange("p (h d) -> p h d", h=8)
        if lat:
            tt("dve", h3(qd), h3(pqk[:, 0:512]), bc(eG.unsqueeze(2), [128, 8, 64]), ALU.mult, r=[r_pqk, r_st5], w=[r_qd])
        tt("dve", h3(kbg), h3(pqk[:, 512:1024]), bc(bEG.unsqueeze(2), [128, 8, 64]), ALU.mult, r=[r_pqk, r_st5], w=[r_kbg])
        tt("dve", h3(ktl), h3(pqk[:, 512:1024]), bc(ktw.unsqueeze(2), [128, 8, 64]), ALU.mult, r=[r_pqk, r_st5], w=[r_ktl])
        tt("dve", h3(vb), h3(pv[:, 0:512]), bc(bd_.unsqueeze(2), [128, 8, 64]), ALU.mult, r=[r_pv, r_g[t]], w=[r_vb])
        QdT, r_QdT = pd["QdT%d" % par]
        if CUT <= 2:
            return
        yield
        C_, r_C = mats["C"]
        intra, r_intra = mats["intra"]
        E_, r_E = halfb["E"]
        kkS, r_kkS = halfb["kkS"]
        nbE, r_nbE = halfb["nbE"]
        for half in range(2):
            hs = slice(half, 8, 2)
            pg, _, r_pg = nb()
            for hh in range(4):
                h = half + 2 * hh
                for pc in range(3):
                    mm(pg[:, hh * 128:(hh + 1) * 128], bc(gp[pc][:, t, d * 8 + h:d * 8 + h + 1], [128, 128]),
                       cB("tri_" + sfx), start=(pc == 0), stop=False, r=[r_cb, r_gp], w=[r_pg])
                mm(pg[:, hh * 128:(hh + 1) * 128], cB("ident"), cB("big_" + sfx), start=False, stop=True,
                   r=[r_cb], w=[r_pg])
            yield
            for hh in range(4):
                h = half + 2 * hh
                act(E_[:, hh, :], pg[:, hh * 128:(hh + 1) * 128], AF.Exp, scale=-1.0, bias=Gc[:, h:h + 1],
                    r=[r_pg, r_st5], w=[r_E])
            yield
            pk, _, r_pk = nb()
            for hh in range(4):
                h = half + 2 * hh
                KTh = pre[(h % 2) * 64:(h % 2) * 64 + 64, 4 + h // 2, tsl]
                mm(pk[:, hh * 128:(hh + 1) * 128], KTh, KTh, r=[r_pre[4 + h // 2]], w=[r_pk])
            yield
            tt("dve", kkS, pk[:, 0:512].rearrange("p (h j) -> p h j", h=4),
               bc(cB("strict_" + sfx).unsqueeze(1), [128, 4, 128]), ALU.mult, r=[r_pk, r_cb], w=[r_kkS])
            tt("pool", nbE, E_, bc(nbeta[:, hs].unsqueeze(2), [128, 4, 128]), ALU.mult, r=[r_E, r_st5], w=[r_nbE])
            tt("pool", C_[:, hs, :], nbE, kkS, ALU.mult, r=[r_nbE, r_kkS], w=[r_C])
            if lat:
                pq, _, r_pq = nb()
                for hh in range(4):
                    h = half + 2 * hh
                    KTh = pre[(h % 2) * 64:(h % 2) * 64 + 64, 4 + h // 2, tsl]
                    QTh = pre[(h % 2) * 64:(h % 2) * 64 + 64, h // 2, tsl]
                    mm(pq[:, hh * 128:(hh + 1) * 128], QTh, KTh, r=[r_pre[4 + h // 2], r_pre[h // 2]], w=[r_pq])
                tt("dve", intra[:, hs, :], E_, pq[:, 0:512].rearrange("p (h j) -> p h j", h=4), ALU.mult,
                   r=[r_E, r_pq], w=[r_intra])
        yield
        if lat:
            _, pq2, r_pq2 = nb()
            for h in range(8):
                tr(pq2[0:64, h * 128:(h + 1) * 128], qd[:, h * 64:(h + 1) * 64], cB("ident"), r=[r_qd, r_cb], w=[r_pq2])
            evac(QdT[0:64], pq2[0:64, :].rearrange("p (h t) -> p h t", h=8), r=[r_pq2], w=[r_QdT])
        for half_ in range(2):
            hs_ = slice(4 * half_, 4 * half_ + 4)
            tt("pool", mats["m0"][0][:, hs_, :], C_[:, hs_, :], bc(cB("m8").unsqueeze(1), [128, 4, 128]), ALU.mult,
               r=[r_C, r_cb], w=[mres[d][0][half_]])
        yield
        B_h, r_B_h = mats["B"]
        _, pt1, r_pt1 = nb()
        for h in range(8):
            tr(pt1[:, h * 128:(h + 1) * 128], C_[:, h, :], cB("ident"), r=[r_C, r_cb], w=[r_pt1])
        evac(B_h, pt1.rearrange("p (h j) -> p h j", h=8), r=[r_pt1], w=[r_B_h])
        for half_ in range(2):
            hs_ = slice(4 * half_, 4 * half_ + 4)
            tt("dve", mats["m1"][0][:, hs_, :], B_h[:, hs_, :], bc(cB("m8").unsqueeze(1), [128, 4, 128]), ALU.mult,
               r=[r_B_h, r_cb], w=[mres[d][1][half_]])
        yield "endA"
        B_, r_B = mats["B"]
        intraT, r_intraT = pd["intraT"]
        if lat:
            _, pt2, r_pt2 = nb()
            for h in range(8):
                tr(pt2[:, h * 128:(h + 1) * 128], intra[:, h, :], cB("ident"), r=[r_intra, r_cb], w=[r_pt2])
            evac(intraT, pt2.rearrange("p (h j) -> p h j", h=8), r=[r_pt2], w=[r_intraT])
        yield
        M = [(mats["m%d" % i][0], mres[d][i]) for i in range(8)]

        def masked(dst, src, mname, eng="pool"):
            (d_, r_d), (s_, r_s) = dst, src
            for half in range(2):
                hs_ = slice(4 * half, 4 * half + 4)
                tt(eng, d_[:, hs_, :], s_[:, hs_, :], bc(cB(mname).unsqueeze(1), [128, 4, 128]), ALU.mult,
                   r=[r_s, r_cb], w=[r_d[half]])

        def plus_ident(dst, src):
            (d_, r_d), (s_, r_s) = dst, src
            for half in range(2):
                hs_ = slice(4 * half, 4 * half + 4)
                tt("dve", d_[:, hs_, :], s_[:, hs_, :], bc(cB("ident").unsqueeze(1), [128, 4, 128]), ALU.add,
                   r=[r_s[half], r_cb], w=[r_d[half]])

        IDENT = (None, None)

        def stage(dst, terms):
            d_, r_d = dst
            mterms = [t_ for t_ in terms if t_[0][0] is not None]
            aterms = [t_ for t_ in terms if t_[0][0] is None]
            for half in range(2):
                pb_, _, r_pb_ = nb()
                for hh in range(4):
                    h = 4 * half + hh
                    for ti, ((l_, r_l), (x_, r_x)) in enumerate(mterms):
                        mm(pb_[:, hh * 128:(hh + 1) * 128], l_[:, h, :], x_[:, h, :], start=(ti == 0), stop=(ti == len(mterms) - 1),
                           r=[r_l[half], r_x[half]], w=[r_pb_])
                dsth = d_[:, 4 * half:4 * half + 4, :]
                psv = pb_[:, 0:512].rearrange("p (h j) -> p h j", h=4)
                if aterms:
                    (_, _), (xa_, r_xa) = aterms[0]
                    tt("dve", dsth, psv, xa_[:, 4 * half:4 * half + 4, :], ALU.add, r=[r_pb_, r_xa[half]], w=[r_d[half]])
                else:
                    cp("act", dsth, psv, r=[r_pb_], w=[r_d[half]])

        Cm, Bm = (C_, r_C), (B_, r_B)
        yield
        stage(M[2], [(M[1], M[0])])
        yield
        stage(M[3], [(M[0], M[1])])
        yield
        plus_ident(M[4], M[0])
        yield
        plus_ident(M[5], M[1])
        yield
        stage(M[6], [(M[3], M[2])])
        yield
        stage(M[7], [(M[2], M[3])])
        yield
        stage(M[0], [(M[3], M[4]), (IDENT, M[4])])
        yield
        stage(M[1], [(M[2], M[5]), (IDENT, M[5])])
        yield
        stage(M[4], [(M[7], M[0]), (IDENT, M[0])])
        yield
        stage(M[5], [(M[6], M[1]), (IDENT, M[1])])
        yield
        X, XT = M[4], M[5]
        freeb = [M[0], M[1]]
        for lvl, mname in enumerate(["l16", "l32", "l64"]):
            masked(M[2], Cm, mname)
            yield
            stage(M[6], [(M[2], XT)])
            yield
            if lvl < 2:
                masked(M[3], Bm, mname)
                yield
                stage(M[7], [(M[3], X)])
                yield
            XTn, Xn = freeb
            stage(XTn, [(X, M[6]), (IDENT, XT)])
            yield
            if lvl < 2:
                stage(Xn, [(XT, M[7]), (IDENT, X)])
                yield
            freeb = [XT, X]
            X, XT = Xn, XTn
        TinvT, r_Ti = XT
        yield
        WT, r_WT = pd["WT"]
        U, r_U = pd["U"]
        for half in range(2):
            hs = slice(4 * half, 4 * half + 4)
            pw, _, r_pw = nb()
            for hh in range(4):
                h = 4 * half + hh
                mm(pw[0:64, hh * 128:(hh + 1) * 128], kbg[:, h * 64:(h + 1) * 64], TinvT[:, h, :], r=[r_kbg, r_Ti[half]], w=[r_pw])
            evac(WT[0:64, hs, :], pw[0:64, 0:512].rearrange("p (h j) -> p h j", h=4), r=[r_pw], w=[r_WT])
        pu, _, r_pu = nb()
        for h in range(8):
            mm(pu[:, h * 64:(h + 1) * 64], TinvT[:, h, :], vb[:, h * 64:(h + 1) * 64], r=[r_Ti[h // 4], r_vb], w=[r_pu])
        evac(U, pu[:, 0:512], r=[r_pu], w=[r_U])
        yield "endB"
        S_, r_S = pd["S"]
        Sbf, r_Sbf = pd["Sbf"]
        ubf, r_ubf = pd["ubf"]
        tmpS, r_tmpS = pd["tmpS"]
        for ch in ([0, 1] if d == 0 else [1, 0]):
            rows = slice(ch * 64, ch * 64 + 64)
            p1, _, r_p1 = nb()
            for h in range(8):
                mm(p1[:, h * 64:(h + 1) * 64], WT[0:64, h, :], Sbf[0:64, h * 64:(h + 1) * 64], r=[r_WT, r_Sbf], w=[r_p1])
            tt("dve", ubf[rows, :], U[rows, :], p1[rows, 0:512], ALU.subtract, r=[r_U, r_p1], w=[r_ubf])
            yield
            if lat:
                l = t - 2
                p2, _, r_p2 = nb()
                for h in range(8):
                    mm(p2[:, h * 64:(h + 1) * 64], QdT[0:64, h, :], Sbf[0:64, h * 64:(h + 1) * 64], start=True, stop=False,
                       r=[r_QdT, r_Sbf], w=[r_p2])
                    mm(p2[:, h * 64:(h + 1) * 64], intraT[:, h, :], ubf[:, h * 64:(h + 1) * 64], start=False, stop=True,
                       r=[r_intraT, r_ubf], w=[r_p2])
                if (l, ch) not in od_seen:
                    od_seen.add((l, ch))
                    cp("act", o_d[rows, l, :], p2[rows, 0:512], r=[r_p2], w=[r_od[l]])
                else:
                    tt("dve", o_d[rows, l, :], o_d[rows, l, :], p2[rows, 0:512], ALU.add, r=[r_p2, r_od[l]], w=[r_od[l]])
            p3, _, r_p3 = nb()
            for h in range(8):
                mm(p3[0:64, h * 64:(h + 1) * 64], ktl[rows, h * 64:(h + 1) * 64], ubf[rows, h * 64:(h + 1) * 64],
                   r=[r_ktl, r_ubf], w=[r_p3])
            tt("pool", h3(tmpS[0:64, :]), h3(S_[0:64, :]), bc(egl[0:64, ch * 8:(ch + 1) * 8].unsqueeze(2), [64, 8, 64]),
               ALU.mult, r=[r_S, r_egl], w=[r_tmpS])
            tt("dve", Sbf[0:64, :], tmpS[0:64, :], p3[0:64, 0:512], ALU.add, r=[r_tmpS, r_p3], w=[r_Sbf])
            tt("dve", S_[0:64, :], tmpS[0:64, :], p3[0:64, 0:512], ALU.add, r=[r_tmpS, r_p3], w=[r_S])
            yield

    fwd_seq = list(range(NT))
    bwd_seq = [1, 0] + list(range(NT - 1, 1, -1))
    nsteps = int(os.environ.get("DN_STEPS", NT))
    def advance(pairs):
        live = list(pairs)
        while live:
            for item in list(live):
                g_, stop = item
                try:
                    v = next(g_)
                except StopIteration:
                    live.remove(item)
                    continue
                if stop is not None and v == stop:
                    live.remove(item)

    visits = [(dn_visit(fwd_seq[s_], 0, s_ % 2), dn_visit(bwd_seq[s_], 1, s_ % 2)) for s_ in range(nsteps)]
    advance([(g_, "endA") for g_ in visits[0]])
    for s_ in range(nsteps):
        advance([(g_, "endB") for g_ in visits[s_]])
        nxt = [(g_, "endA") for g_ in visits[s_ + 1]] if s_ + 1 < nsteps else []
        advance(nxt + [(g_, None) for g_ in visits[s_]])
        if s_ == 1 and "Sctx" in debug:
            dump("Sf", per[0]["S"][0][0:64], per[0]["S"][1])
            dump("Sb", per[1]["S"][0][0:64], per[1]["S"][1])
    if "o_d" in debug:
        r_all = Res("dall4")
        S.wait_only("pool", [r.w for r in r_od])
        dump("o_d", o_d, r_all)
    for d in range(2):
        for nm in per[d]:
            A.free("d%d_%s" % (d, nm))
    for d in range(2):
        for nm in matsD[d]:
            A.free("m%d_%s" % (d, nm))
        for nm in halfbD[d]:
            A.free("h%d_%s" % (d, nm))
        for nm in tokbD[d]:
            A.free("t%d_%s" % (d, nm))
        A.free("st5_%d" % d)
    A.free("pre")
    if stop_after == 5:
        return finish(nc, S, out_d, dbg_out, None)
    S.barrier()
    A.free("gall", "ball", "gp0", "gp1", "gp2")

    NW = 4
    wbra = A.alloc("wbra", [128, 4, D], BF16)
    wbrd = A.alloc("wbrd", [128, 4, D], BF16)
    r_wbra, r_wbrd = Res("wbra"), Res("wbrd")
    wpool = [A.alloc("wp%d" % i, [128, 4096], BF16) for i in range(NW)]
    r_wpool = [Res("wp%d" % i) for i in range(NW)]
    wpi = [0]

    def wslab(src, a):
        i = wpi[0] % NW
        wpi[0] += 1
        v = wpool[i].rearrange("p (a b) -> p a b", a=a)
        S.dma("pool", [(v, src)], w=[r_wpool[i]])
        return v, r_wpool[i]

    w1_v = w1_d.rearrange("(k p) n -> p k n", p=128)
    w2_v = w2_d.rearrange("(f p) n -> p f n", p=128)
    wout_v = wout_d.rearrange("(k p) n -> p k n", p=128)
    wbra_v = wbra_d.rearrange("(k p) n -> p k n", p=128)
    wbrd_v = wbrd_d.rearrange("(k p) n -> p k n", p=128)

    onec = salloc(1)
    memset("pool", onec, 1.0, w=[r_small])
    xall = A.alloc("xall", [128, 16, D], F32)
    r_xall = [Res("xall%d" % i) for i in range(16)]
    hTb = A.alloc("hTb", [128, 8, 512], BF16)
    r_hTb = [Res("hTb%d" % i) for i in range(4)]
    ydT = A.alloc("ydT", [128, 4, 512], BF16)
    r_ydT = [Res("ydT%d" % i) for i in range(4)]
    yTb = A.alloc("yTb", [128, 8, 512], BF16)
    r_yTb = [Res("yTb%d" % f) for f in range(8)]
    NS6 = 4
    es6 = [(A.alloc("f_e%d" % i, [128, 512], F32), Res("f_e%d" % i)) for i in range(NS6)]
    ts6 = [(A.alloc("f_t%d" % i, [128, 512], F32), Res("f_t%d" % i)) for i in range(NS6)]
    ydts = [(A.alloc("ydt%d" % i, [128, 512], BF16), Res("ydt%d" % i)) for i in range(2)]
    junk6 = A.alloc("junk6", [128, D], BF16)
    r_junk6 = Res("junk6")
    xn6 = [A.alloc("xn6_%d" % i, [128, D], BF16) for i in range(2)]
    r_xn6 = [Res("xn6_%d" % i) for i in range(2)]
    st6 = A.alloc("st6", [128, 32], F32)
    r_st6 = Res("st6")

    def sigmoid_act(dst, r_dst, src_ps, r_src):
        act(dst, src_ps, AF.Exp, scale=-1.0, r=[r_src], w=[r_dst])
        act(dst, dst, AF.Ln, bias=onec, r=[r_dst, r_small], w=[r_dst])
        act(dst, dst, AF.Exp, scale=-1.0, r=[r_dst], w=[r_dst])

    nblk = int(os.environ.get("P6_BLOCKS", 4))
    for b in range(nblk):
        items = []
        for i in range(4):
            lt = 4 * b + i
            items.append(dict(src=xall[:, lt, :], r_src=r_xall[lt], dstT=hTb, tok=slice(i * 128, (i + 1) * 128), r_dst=r_hTb[i],
                              A=coef[:, 0, :], B=(lambda c: modv[:, c, 0:1]), bufs=(junk6, r_junk6, xn6[i % 2], r_xn6[i % 2]), i=lt,
                              pre=(lambda lt=lt: S.dma("sp", [(xall[:, lt, :], x_d[lt * 128:(lt + 1) * 128, :])], w=[r_xall[lt]]))))
        norm_transpose_seq(items)
        wz, r_wz = wslab(win_v[:, :, 2304:2816], 8)
        pszs = {}

        def z_proj(i):
            psz, _, r_psz = nb()
            for k in range(8):
                mm(psz[:, 0:512], hTb[:, k, i * 128:(i + 1) * 128], wz[:, k, :], start=(k == 0), stop=(k == 7),
                   r=[r_hTb[i], r_wz], w=[r_psz])
            pszs[i] = (psz, r_psz)

        z_proj(0)
        for i in range(4):
            lt = 4 * b + i
            if i + 1 < 4:
                z_proj(i + 1)
            sa, sb_ = 2 * (i % 2), 2 * (i % 2) + 1
            e1, r_e1 = es6[sa]
            t1, r_t1 = ts6[sa]
            sq6, r_sq6 = es6[sb_]
            t2, r_t2 = ts6[sb_]
            ydt, r_ydt = ydts[i % 2]
            psz, r_psz = pszs[i]
            sigmoid_act(e1, r_e1, psz[:, 0:512], r_psz)
            tt("dve", e1, psz[:, 0:512], e1, ALU.mult, r=[r_psz, r_e1], w=[r_e1])
            od3 = o_d[:, lt, :].rearrange("p (h d) -> p h d", h=8)
            act(sq6, o_d[:, lt, :], AF.Square, r=[r_od[lt]], w=[r_sq6])
            ssq = st6[:, 0:8]
            S.op("dve", lambda e, ssq=ssq, sq6=sq6: e.tensor_reduce(ssq, sq6.rearrange("p (h d) -> p h d", h=8), AX.X, ALU.add),
                 r=[r_sq6], w=[r_st6])
            rstd_from_ss(st6[:, 16:24], ssq, 1.0 / 64, st6[:, 8:16], r_st6, r_st6, r_st6)
            tt("dve", t1.rearrange("p (h d) -> p h d", h=8), od3, bc(st6[:, 16:24].unsqueeze(2), [128, 8, 64]), ALU.mult,
               r=[r_od[lt], r_st6], w=[r_t1])
            tt("dve", t2.rearrange("p (h d) -> p h d", h=8), t1.rearrange("p (h d) -> p h d", h=8),
               bc(dng_b.unsqueeze(1), [128, 8, 64]), ALU.mult, r=[r_t1, r_small], w=[r_t2])
            tt("dve", ydt, t2, e1, ALU.mult, r=[r_t2, r_e1], w=[r_ydt])
            _, pbb, r_pb = nb()
            for c in range(4):
                tr(pbb[:, c * 128:(c + 1) * 128], ydt[:, c * 128:(c + 1) * 128], cB("ident"), r=[r_ydt, r_cb], w=[r_pb])
            cp("act", ydT[:, :, i * 128:(i + 1) * 128], pbb[:, 0:512].rearrange("p (c t) -> p c t", c=4),
               r=[r_pb], w=[r_ydT[i]])
        bsl = slice(b * 512, (b + 1) * 512)
        for f in range(8):
            if f % 4 == 0:
                wga, r_wga = wslab(win_v[:, :, 2848 + f * 128:2848 + f * 128 + 512], 8)
                wgd, r_wgd = wslab(win_v[:, :, 3872 + f * 128:3872 + f * 128 + 512], 8)
            if f == 0 and b == 0:
                S.dma("pool", [(wbra, wbra_v)], w=[r_wbra])
                S.dma("pool", [(wbrd, wbrd_v)], w=[r_wbrd])
            fc = (f % 4) * 128
            sa, sb_ = (2 * f) % NS6, (2 * f + 1) % NS6
            (e1, r_e1), (t1, r_t1) = es6[sa], ts6[sa]
            (e2, r_e2), (t2, r_t2) = es6[sb_], ts6[sb_]
            for (wg, r_wg, wb_, r_wb_, src, r_src, eb, r_eb, tb_, r_tb) in [
                    (wga, r_wga, wbra, r_wbra, yattnT[:, :, bsl], r_yat[4 * b:4 * b + 4], e1, r_e1, t1, r_t1),
                    (wgd, r_wgd, wbrd, r_wbrd, ydT, r_ydT, e2, r_e2, t2, r_t2)]:
                psg_, _, r_psg_ = nb()
                for k in range(8):
                    mm(psg_[:, 0:512], wg[:, k, fc:fc + 128], hTb[:, k, :], start=(k == 0), stop=(k == 7),
                       r=[r_wg] + r_hTb, w=[r_psg_])
                act(eb, psg_[:, 0:512], AF.Sigmoid, r=[r_psg_], w=[r_eb])
                psb_, _, r_psb_ = nb()
                for k in range(4):
                    mm(psb_[:, 0:512], wb_[:, k, f * 128:(f + 1) * 128], src[:, k, :], start=(k == 0), stop=(k == 3),
                       r=[r_wb_] + list(r_src), w=[r_psb_])
                tt("dve", tb_, psb_[:, 0:512], eb, ALU.mult, r=[r_psb_, r_eb], w=[r_tb])
            tt("dve", yTb[:, f, :], t1, t2, ALU.add, r=[r_t1, r_t2], w=[r_yTb[f]])
        for half in range(2):
            wo, r_wo = wslab(wout_v[:, :, half * 512:(half + 1) * 512], 8)
            for i in range(4):
                lt = 4 * b + i
                pso_, _, r_pso_ = nb()
                for f in range(8):
                    mm(pso_[:, 0:512], yTb[:, f, i * 128:(i + 1) * 128], wo[:, f, :], start=(f == 0), stop=(f == 7),
                       r=[r_yTb[f], r_wo], w=[r_pso_])
                t1, r_t1 = ts6[(half * 4 + i) % NS6]
                tt("dve", t1, pso_[:, 0:512], gtrow[:, 0, half * 512:(half + 1) * 512], ALU.mult,
                   r=[r_pso_, r_gtrow], w=[r_t1])
                xs = xall[:, lt, half * 512:(half + 1) * 512]
                tt("dve", xs, xs, t1, ALU.add, r=[r_t1, r_xall[lt]], w=[r_xall[lt]])
    if "xnew_all" in debug:
        r_all = Res("dall7")
        S.wait_only("pool", [r.w for r in r_xall])
        dump("xnew_all", xall, r_all)
    S.barrier()
    A.free("wbra", "wbrd", "hTb", "ydT", "yTb", "ydt0", "ydt1", "yattnT", "o_d", "f_t0", "f_t1", "f_t2", "f_t3", "f_e2", "f_e3")

    h2Th = [A.alloc("h2T%d" % i, [128, 8, SEQ // 2], BF16) for i in range(2)]
    r_h2T = [Res("h2T%d" % i) for i in range(16)]
    m1b = [[A.alloc("m1b%d_%d" % (i, tb), [128, 4, 512], BF16) for tb in range(4)] for i in range(2)]
    r_m1b = [[Res("m1b%d_%d" % (i, tb)) for tb in range(4)] for i in range(2)]
    items = []
    for lt in range(16):
        items.append(dict(src=xall[:, lt, :], r_src=r_xall[lt], dstT=h2Th[lt // 8], tok=slice((lt % 8) * 128, (lt % 8 + 1) * 128),
                          r_dst=r_h2T[lt], A=coef[:, 2, :], B=(lambda c: modv[:, 24 + c, 0:1]),
                          bufs=(junk6, r_junk6, xn6[lt % 2], r_xn6[lt % 2]), i=lt))
    norm_transpose_seq(items)
    e1, r_e1 = es6[0]
    e2, r_e2 = es6[1]
    out_toks = []
    slabs = {}

    def mlp1(j):
        w1s, r_w1s = wslab(w1_v[:, :, j * 512:(j + 1) * 512], 8)
        w2s, r_w2s = wslab(w2_v[:, 4 * j:4 * j + 4, :], 4)
        tt("dve", w2s, w2s, bc(gtrow[:, 1, :].unsqueeze(1), [128, 4, D]), ALU.mult, r=[r_w2s, r_gtrow], w=[r_w2s])
        slabs[j] = (w2s, r_w2s)
        m1j, r_m1j = m1b[j % 2], r_m1b[j % 2]
        for tb in range(4):
            for cc in range(4):
                ps1, _, r_ps1 = nb()
                for k in range(8):
                    mm(ps1[:, 0:512], w1s[:, k, cc * 128:(cc + 1) * 128], h2Th[tb // 2][:, k, (tb % 2) * 512:(tb % 2 + 1) * 512],
                       start=(k == 0), stop=(k == 7), r=[r_w1s] + r_h2T[4 * tb:4 * tb + 4], w=[r_ps1])
                sqb_, r_sqb_ = (e1, r_e1) if cc % 2 == 0 else (e2, r_e2)
                act(sqb_, ps1[:, 0:512], AF.Square, r=[r_ps1], w=[r_sqb_])
                stt("dve", m1j[tb][:, cc, :], ps1[:, 0:512], 0.0, sqb_, ALU.is_gt, ALU.mult,
                    r=[r_ps1, r_sqb_], w=[r_m1j[tb]])

    def mlp2(j):
        w2s, r_w2s = slabs[j]
        m1j, r_m1j = m1b[j % 2], r_m1b[j % 2]
        for lt in range(16):
            for half in range(2):
                ps2, _, r_ps2 = nb()
                for cc in range(4):
                    mm(ps2[:, 0:512], m1j[lt // 4][:, cc, (lt % 4) * 128:(lt % 4 + 1) * 128], w2s[:, cc, half * 512:(half + 1) * 512],
                       start=(cc == 0), stop=(cc == 3), r=[r_m1j[lt // 4], r_w2s], w=[r_ps2])
                xs = xall[:, lt, half * 512:(half + 1) * 512]
                tt("dve", xs, xs, ps2[:, 0:512], ALU.add, r=[r_ps2, r_xall[lt]], w=[r_xall[lt]])
            if j == 7:
                out_toks.append(S.dma("sp", [(out_d[lt * 128:(lt + 1) * 128, :], xall[:, lt, :])], r=[r_xall[lt]]))

    mlp1(0)
    for j in range(8):
        if j + 1 < 8:
            mlp1(j + 1)
        mlp2(j)
    return finish(nc, S, out_d, dbg_out, out_toks)


def finish(nc, S, out_d, dbg_out, out_toks):
    if out_toks:
        S.wait_only("sp", out_toks)
    S.barrier()
    n = S.emit()
    DBG["ninst"] = n
    DBG["nsem"] = S.nsem
    DBG["dbg_out"] = dbg_out
    return nc


def make_in_maps(inputs):
    f = lambda a: np.ascontiguousarray(np.asarray(a, dtype=np.float32))
    x = f(inputs["x"]); c = f(inputs["c"]); ctx = f(inputs["ctx"]); c_ctx = f(inputs["c_ctx"])
    B = x.shape[0]
    shared = {
        "badaT": f(inputs["b_ada"][0].reshape(48, 128).T),
        "bada": f(inputs["b_ada"][0]),
        "g1T": f(inputs["g_norm1"][0].reshape(8, 128).T),
        "g2T": f(inputs["g_norm2"][0].reshape(8, 128).T),
        "w_ada": f(inputs["w_ada"][0]),
        "w_in": f(inputs["w_in"][0]),
        "q_norm_g": f(inputs["q_norm_g"][0]),
        "k_norm_g": f(inputs["k_norm_g"][0]),
        "attn_sink": f(inputs["attn_sink"][0]),
        "convT": f(np.asarray(inputs["conv_w"][0]).reshape(3, 12, 128).transpose(2, 1, 0).reshape(128, 36)),
        "a_log": f(np.asarray(inputs["a_log"][0]).reshape(16)),
        "dt_bias": f(np.asarray(inputs["dt_bias"][0]).reshape(16)),
        "dn_norm_g": f(inputs["dn_norm_g"][0]),
        "w_br_attn": f(inputs["w_br_attn"][0]),
        "w_br_dn": f(inputs["w_br_dn"][0]),
        "w_out": f(inputs["w_out"][0]),
        "w_mlp1": f(inputs["w_mlp1"][0]),
        "w_mlp2": f(inputs["w_mlp2"][0]),
        "consts": CONST_ARR,
        "rope_cos": ROPE_COS,
        "rope_sin": ROPE_SIN,
    }
    maps = []
    for b in range(B):
        m = dict(shared)
        m["x"] = x[b]
        m["ctx"] = ctx[b]
        cc = np.stack([c[b].reshape(8, 128), c_ctx.reshape(8, 128)], axis=-1)
        m["cT"] = f(cc.transpose(1, 0, 2).reshape(128, 16))
        maps.append(m)
    return maps


def kernel(**inputs):
    maps = make_in_maps(inputs)
    nc = build()
    res = run_bass_kernel_spmd(nc, maps, core_ids=list(range(8)))
    out = np.stack([np.asarray(r["out"]) for r in res.results], axis=0)
    return out.astype(np.float32)
```
